# Optimizing a Trainium2 kernel written in Bass

```python
import math
import jax, jax.numpy as jnp
from jax import lax
import numpy as np

D_MODEL = 2048
BATCH = 4
SEQ = 2048
DEPTH = 2
DEC_BATCH = 128
DEC_SEQ = 4
PAST_LEN = 16384
PAGE_SIZE = 128

N_META = 16
D_FF = 5632
CONV_KERNEL = 31
CONV_BUF = CONV_KERNEL - 1
D_CONV = D_MODEL
GLA_HEADS = 4
GLA_DK = D_MODEL // 2 // GLA_HEADS
GLA_DV = D_MODEL // GLA_HEADS
GLA_GATE_RANK = 16
GLA_TAU = 16.0
GLA_CHUNK = 64
N_CONV_LAYERS = (DEPTH + 1) // 2
N_GLA_LAYERS = DEPTH // 2
EPS = 1e-6

kernel_name = "conformer_conv_gla_macaron_hybrid_step"


def rms_norm(x, g):
    xf = x.astype(jnp.float32)
    y = xf * lax.rsqrt(jnp.mean(xf * xf, axis=-1, keepdims=True) + EPS)
    return (y * g.astype(jnp.float32)).astype(x.dtype)


def swiglu_ffn(h, w_gate, w_up, w_down):
    return (jax.nn.silu(h @ w_gate) * (h @ w_up)) @ w_down


def conv_module(h, buf, w_pw1, b_pw1, w_dw, b_dw, ln_g, ln_b, w_pw2, b_pw2):
    a = h @ w_pw1 + b_pw1
    u = a[..., :D_CONV] * jax.nn.sigmoid(a[..., D_CONV:])
    ext = jnp.concatenate([buf.astype(u.dtype), u], axis=1)
    y = lax.conv_general_dilated(
        ext, w_dw.astype(u.dtype)[:, None, :], window_strides=(1,), padding='VALID',
        dimension_numbers=('NWC', 'WIO', 'NWC'), feature_group_count=D_CONV) + b_dw
    yf = y.astype(jnp.float32)
    mu = jnp.mean(yf, axis=-1, keepdims=True)
    var = jnp.mean(jnp.square(yf - mu), axis=-1, keepdims=True)
    yn = ((yf - mu) * lax.rsqrt(var + EPS) * ln_g.astype(jnp.float32) + ln_b.astype(jnp.float32)).astype(h.dtype)
    out = jax.nn.silu(yn) @ w_pw2 + b_pw2
    return out, ext[:, -CONV_BUF:].astype(buf.dtype)


def gla_chunk(S0, q, k, v, la):
    L = q.shape[2]
    b = jnp.cumsum(la, axis=2)
    inter = jnp.einsum('bhtd,bhdv->bhtv', q * jnp.exp(b), S0)
    mask = jnp.tril(jnp.ones((L, L), dtype=bool))
    diff = b[:, :, :, None, :] - b[:, :, None, :, :]
    decay = jnp.exp(jnp.where(mask[:, :, None], diff, -jnp.inf))
    scores = jnp.einsum('bhtsd,bhsd->bhts', q[:, :, :, None, :] * decay, k)
    intra = jnp.einsum('bhts,bhsv->bhtv', scores, v)
    b_last = b[:, :, -1:, :]
    S = jnp.exp(b_last[:, :, 0, :])[..., None] * S0 + jnp.einsum(
        'bhsd,bhsv->bhdv', k * jnp.exp(b_last - b), v)
    return S, inter + intra


def gla_module(h, S0, lead, w_q, w_k, w_v, w_g1, w_g2, b_g, w_r, gn_g, w_o):
    B, L, _ = h.shape
    f32 = jnp.float32

    def heads(t, d):
        return t.reshape(B, L, GLA_HEADS, d).transpose(0, 2, 1, 3).astype(f32)

    q = heads(h @ w_q, GLA_DK) * (GLA_DK ** -0.5)
    k = heads(h @ w_k, GLA_DK)
    v = heads(h @ w_v, GLA_DV)
    la = jax.nn.log_sigmoid(heads((h @ w_g1) @ w_g2 + b_g, GLA_DK)) / GLA_TAU
    S = S0.astype(f32)
    parts = []
    if lead > 0:
        S, o_lead = gla_chunk(S, q[:, :, :lead], k[:, :, :lead], v[:, :, :lead], la[:, :, :lead])
        parts.append(o_lead)
    rest = L - lead
    c = math.gcd(GLA_CHUNK, rest)
    n = rest // c

    def to_blocks(t):
        return t[:, :, lead:].reshape(B, GLA_HEADS, n, c, t.shape[-1]).transpose(2, 0, 1, 3, 4)

    def step(S_c, xs):
        return gla_chunk(S_c, *xs)

    S, o_rest = lax.scan(step, S, (to_blocks(q), to_blocks(k), to_blocks(v), to_blocks(la)))
    parts.append(o_rest.transpose(1, 2, 0, 3, 4).reshape(B, GLA_HEADS, rest, GLA_DV))
    o = jnp.concatenate(parts, axis=2).transpose(0, 2, 1, 3)
    o = o * lax.rsqrt(jnp.mean(o * o, axis=-1, keepdims=True) + EPS) * gn_g.astype(f32)
    r = jax.nn.silu((h @ w_r).astype(f32)).reshape(B, L, GLA_HEADS, GLA_DV)
    out = (o * r).reshape(B, L, GLA_HEADS * GLA_DV).astype(h.dtype) @ w_o
    return out, S.astype(S0.dtype)


def trunk(x, conv_bufs, gla_states, lead, p):
    new_conv, new_gla = [], []
    for i in range(DEPTH):
        j = i // 2
        x = x + 0.5 * swiglu_ffn(rms_norm(x, p['norm_ffn'][i, 0]), p['w_ffn_gate'][i, 0],
                                 p['w_ffn_up'][i, 0], p['w_ffn_down'][i, 0])
        h = rms_norm(x, p['norm_mix'][i])
        if i % 2 == 0:
            out, st = conv_module(h, conv_bufs[j], p['conv_w_pw1'][j], p['conv_b_pw1'][j],
                                  p['conv_w_dw'][j], p['conv_b_dw'][j], p['conv_ln_g'][j],
                                  p['conv_ln_b'][j], p['conv_w_pw2'][j], p['conv_b_pw2'][j])
            new_conv.append(st)
        else:
            out, st = gla_module(h, gla_states[j], lead, p['gla_w_q'][j], p['gla_w_k'][j],
                                 p['gla_w_v'][j], p['gla_w_g1'][j], p['gla_w_g2'][j],
                                 p['gla_b_g'][j], p['gla_w_r'][j], p['gla_gn_g'][j], p['gla_w_o'][j])
            new_gla.append(st)
        x = x + out
        x = x + 0.5 * swiglu_ffn(rms_norm(x, p['norm_ffn'][i, 1]), p['w_ffn_gate'][i, 1],
                                 p['w_ffn_up'][i, 1], p['w_ffn_down'][i, 1])
    return rms_norm(x, p['norm_final']), jnp.stack(new_conv), jnp.stack(new_gla)


def setup_inputs(seed: int = 0) -> dict:
    key = jax.random.key(seed)
    ks = iter(jax.random.split(key, 32))
    f32 = jnp.float32

    def nrm(shape, scale):
        return jax.random.normal(next(ks), shape, f32) * scale

    Nc, Ng, D = N_CONV_LAYERS, N_GLA_LAYERS, D_MODEL
    return {
        'x_prompt': nrm((BATCH, SEQ, D), 1.0),
        'x_sample': nrm((DEC_BATCH, DEC_SEQ, D), 1.0),
        'state_conv': nrm((Nc, DEC_BATCH, CONV_BUF, D_CONV), 0.5),
        'state_gla': nrm((Ng, DEC_BATCH, GLA_HEADS, GLA_DK, GLA_DV), 1.0),
        'meta_tokens': nrm((N_META, D), 1.0),
        'norm_ffn': 1.0 + nrm((DEPTH, 2, D), 0.02),
        'w_ffn_gate': nrm((DEPTH, 2, D, D_FF), D ** -0.5),
        'w_ffn_up': nrm((DEPTH, 2, D, D_FF), D ** -0.5),
        'w_ffn_down': nrm((DEPTH, 2, D_FF, D), D_FF ** -0.5),
        'norm_mix': 1.0 + nrm((DEPTH, D), 0.02),
        'conv_w_pw1': nrm((Nc, D, 2 * D_CONV), D ** -0.5),
        'conv_b_pw1': nrm((Nc, 2 * D_CONV), 0.02),
        'conv_w_dw': nrm((Nc, CONV_KERNEL, D_CONV), CONV_KERNEL ** -0.5),
        'conv_b_dw': nrm((Nc, D_CONV), 0.02),
        'conv_ln_g': 1.0 + nrm((Nc, D_CONV), 0.02),
        'conv_ln_b': nrm((Nc, D_CONV), 0.02),
        'conv_w_pw2': nrm((Nc, D_CONV, D), D_CONV ** -0.5),
        'conv_b_pw2': nrm((Nc, D), 0.02),
        'gla_w_q': nrm((Ng, D, GLA_HEADS * GLA_DK), D ** -0.5),
        'gla_w_k': nrm((Ng, D, GLA_HEADS * GLA_DK), D ** -0.5),
        'gla_w_v': nrm((Ng, D, GLA_HEADS * GLA_DV), D ** -0.5),
        'gla_w_g1': nrm((Ng, D, GLA_GATE_RANK), D ** -0.5),
        'gla_w_g2': nrm((Ng, GLA_GATE_RANK, GLA_HEADS * GLA_DK), GLA_GATE_RANK ** -0.5),
        'gla_b_g': nrm((Ng, GLA_HEADS * GLA_DK), 0.02),
        'gla_w_r': nrm((Ng, D, GLA_HEADS * GLA_DV), D ** -0.5),
        'gla_gn_g': 1.0 + nrm((Ng, GLA_DV), 0.02),
        'gla_w_o': nrm((Ng, GLA_HEADS * GLA_DV, D), (GLA_HEADS * GLA_DV) ** -0.5),
        'norm_final': 1.0 + nrm((D,), 0.02),
    }


def reference(x_prompt, x_sample, state_conv, state_gla, meta_tokens, norm_ffn, w_ffn_gate,
              w_ffn_up, w_ffn_down, norm_mix, conv_w_pw1, conv_b_pw1, conv_w_dw, conv_b_dw,
              conv_ln_g, conv_ln_b, conv_w_pw2, conv_b_pw2, gla_w_q, gla_w_k, gla_w_v, gla_w_g1,
              gla_w_g2, gla_b_g, gla_w_r, gla_gn_g, gla_w_o, norm_final):
    p = dict(norm_ffn=norm_ffn, w_ffn_gate=w_ffn_gate, w_ffn_up=w_ffn_up, w_ffn_down=w_ffn_down,
             norm_mix=norm_mix, conv_w_pw1=conv_w_pw1, conv_b_pw1=conv_b_pw1, conv_w_dw=conv_w_dw,
             conv_b_dw=conv_b_dw, conv_ln_g=conv_ln_g, conv_ln_b=conv_ln_b, conv_w_pw2=conv_w_pw2,
             conv_b_pw2=conv_b_pw2, gla_w_q=gla_w_q, gla_w_k=gla_w_k, gla_w_v=gla_w_v,
             gla_w_g1=gla_w_g1, gla_w_g2=gla_w_g2, gla_b_g=gla_b_g, gla_w_r=gla_w_r,
             gla_gn_g=gla_gn_g, gla_w_o=gla_w_o, norm_final=norm_final)
    B = x_prompt.shape[0]
    meta = jnp.broadcast_to(meta_tokens.astype(x_prompt.dtype)[None], (B, N_META, D_MODEL))
    xp = jnp.concatenate([meta, x_prompt], axis=1)
    conv0 = [jnp.zeros((B, CONV_BUF, D_CONV), state_conv.dtype) for _ in range(N_CONV_LAYERS)]
    gla0 = [jnp.zeros((B, GLA_HEADS, GLA_DK, GLA_DV), state_gla.dtype) for _ in range(N_GLA_LAYERS)]
    yp, new_conv_prompt, new_gla_prompt = trunk(xp, conv0, gla0, N_META, p)
    y_prompt = yp[:, N_META:]
    conv_s = [state_conv[j] for j in range(N_CONV_LAYERS)]
    gla_s = [state_gla[j] for j in range(N_GLA_LAYERS)]
    y_sample, new_conv_sample, new_gla_sample = trunk(x_sample, conv_s, gla_s, 0, p)
    return (y_prompt, y_sample, new_conv_prompt, new_gla_prompt, new_conv_sample, new_gla_sample)
```

```python
import os
import numpy as np
import concourse.bass as bass
import concourse.mybir as mybir
from concourse.bass_utils import run_bass_kernel_spmd

F32 = mybir.dt.float32
BF16 = mybir.dt.bfloat16
AF = mybir.ActivationFunctionType
ALU = mybir.AluOpType

D = 2048
KC = 16
DFF = 5632
NG = 11
TM, TP, TS = 1024, 32, 64
T = TM + TP + TS
CP, CM, CS = 0, TP, TP + TM
CONV_BLOCKS = [(0, 352, TP), (352, 352, 0), (704, 352, 0)]
GLA_BLOCKS = [(0, 288, ((0, 32, True), (32, 128, False), (160, 128, False)))] + \
             [(288 + i * 256, 256, ((0, 128, False), (128, 128, False))) for i in range(3)]
_BND = sorted({0, T} | {b[0] for b in CONV_BLOCKS} | {b[0] for b in GLA_BLOCKS} | {512, 1024, CS})
SEGS = [(_BND[i], _BND[i + 1] - _BND[i]) for i in range(len(_BND) - 1)]


def segs_of(c0, n):
    return tuple(i for i, (a, w) in enumerate(SEGS) if a < c0 + n and a + w > c0)


BLKS = [(0, 512, segs_of(0, 512)), (512, 512, segs_of(512, 512)), (1024, T - 1024, segs_of(1024, T - 1024))]
NSLOT = 6
UE = 4096
EPS = 1e-6
NQ = 8
STAGES = os.environ.get("MK_STAGES", "all")


class Buf:
    __slots__ = ("w", "r")

    def __init__(self):
        self.w = None
        self.r = {}


def bufs(n):
    return [Buf() for _ in range(n)]


class Sched:
    def __init__(self, nc):
        self.nc = nc
        self.eng = {"pe": nc.tensor, "act": nc.scalar, "dve": nc.vector, "pool": nc.gpsimd, "sp": nc.sync}
        self.semh = {}
        self.cnt = {}
        self.seen = {e: {} for e in self.eng}
        self._cms = []
        for e in self.eng:
            self._newsem(e)
        self.dq = {}
        for q in ("sp", "pool", "act"):
            names = []
            for i in range(NQ):
                nm = "d_%s_%d" % (q, i)
                self._newsem(nm)
                names.append(nm)
            self.dq[q] = [names, 0]
        self.out_events = []

    def _newsem(self, name):
        cm = self.nc.semaphore("s_" + name)
        h = cm.__enter__()
        self._cms.append(cm)
        self.semh[name] = h
        self.cnt[name] = 0

    def close(self):
        for cm in reversed(self._cms):
            cm.__exit__(None, None, None)

    def _deps(self, reads, writes):
        deps = {}
        for b in reads:
            if b.w is not None:
                s, v = b.w
                if deps.get(s, 0) < v:
                    deps[s] = v
        for b in writes:
            if b.w is not None:
                s, v = b.w
                if deps.get(s, 0) < v:
                    deps[s] = v
            for s, v in b.r.items():
                if deps.get(s, 0) < v:
                    deps[s] = v
        return deps

    def _wait(self, e, deps):
        eng = self.eng[e]
        seen = self.seen[e]
        for s, v in deps.items():
            if seen.get(s, 0) < v:
                eng.wait_ge(self.semh[s], v)
                seen[s] = v

    def wait_event(self, e, ev):
        self._wait(e, {ev[0]: ev[1]})

    def _mark(self, ev, reads, writes):
        s, v = ev
        for b in writes:
            b.w = ev
            b.r = {}
        for b in reads:
            if b.r.get(s, 0) < v:
                b.r[s] = v

    def op(self, e, fn, reads=(), writes=()):
        self._wait(e, self._deps(reads, writes))
        ins = fn(self.eng[e])
        self.cnt[e] += 1
        ins.then_inc(self.semh[e], 1)
        ev = (e, self.cnt[e])
        self._mark(ev, reads, writes)
        return ev

    def mm(self, fns, reads=(), writes=()):
        deps = self._deps(reads, writes)
        deps.pop("pe", None)
        self._wait("pe", deps)
        ins = None
        for fn in fns:
            ins = fn(self.nc.tensor)
        self.cnt["pe"] += 1
        ins.then_inc(self.semh["pe"], 1)
        ev = ("pe", self.cnt["pe"])
        self._mark(ev, reads, writes)
        return ev

    def dma(self, q, out, in_, reads=(), writes=(), is_output=False):
        names, idx = self.dq[q]
        nm = names[idx % NQ]
        self.dq[q][1] = idx + 1
        deps = self._deps(reads, writes)
        if self.cnt[nm] > 0:
            deps[nm] = max(deps.get(nm, 0), self.cnt[nm])
        self._wait(q, deps)
        self.cnt[nm] += 16
        self.eng[q].dma_start(out=out, in_=in_).then_inc(self.semh[nm], 16)
        ev = (nm, self.cnt[nm])
        self._mark(ev, reads, writes)
        if is_output:
            self.out_events.append(ev)
        return ev

    def barrier(self):
        evs = {e: self.cnt[e] for e in ("pe", "act", "dve") if self.cnt[e] > 0}
        for q in self.dq:
            for nm in self.dq[q][0]:
                if self.cnt[nm] > 0:
                    evs[nm] = self.cnt[nm]
        for e in ("pe", "act", "dve", "sp"):
            self._wait(e, dict(evs))


class Ring:
    def __init__(self, S, ring_tile):
        self.S = S
        self.tile = ring_tile
        self.specs = []
        self.n_issued = 0
        self.n_acq = 0
        self.sems = []
        for i in range(NSLOT):
            nm = "ring%d" % i
            S._newsem(nm)
            self.sems.append(nm)
        self.plan = None
        self.rel_ev = {}
        self.fill_val = {}
        self.conv_jobs = []
        self.conv_total = 0
        S._newsem("wconv")
        self.live = []

    def set_plan(self, plan, scratch_flags, conv_jobs):
        self.plan = plan
        self.plan_scratch = scratch_flags
        self.conv_jobs = list(conv_jobs)

    def _issue(self, u):
        S = self.S
        slot = u % NSLOT
        if u >= NSLOT:
            S.wait_event("pool", self.rel_ev[u - NSLOT])
        nm = self.sems[slot]
        if self.plan_scratch[u]:
            assert not self.conv_jobs, "scratch unit issued before all conversions were queued"
            S.wait_event("pool", ("wconv", S.cnt["wconv"]))
        for (src_ap, dst_fn) in self.plan[u]:
            S.cnt[nm] += 16
            S.nc.gpsimd.dma_start(out=dst_fn(self.tile, slot), in_=src_ap).then_inc(S.semh[nm], 16)
        self.fill_val[u] = S.cnt[nm]
        if u >= NSLOT and self.conv_jobs:
            dst, src = self.conv_jobs.pop(0)
            S.cnt["wconv"] += 16
            S.nc.gpsimd.dma_start(out=dst, in_=src).then_inc(S.semh["wconv"], 16)

    def prime(self):
        while self.n_issued < min(NSLOT, len(self.plan)):
            self._issue(self.n_issued)
            self.n_issued += 1

    def acquire(self):
        u = self.n_acq
        assert u < len(self.plan), "ring plan exhausted"
        self.n_acq += 1
        slot = u % NSLOT
        assert u in self.fill_val, "unit acquired before its DMA was issued"
        self.S.wait_event("pe", (self.sems[slot], self.fill_val[u]))
        self.live.append(u)
        return self.tile, slot

    def release_all(self):
        ev = ("pe", self.S.cnt["pe"])
        for u in self.live:
            self.rel_ev[u] = ev
        self.live = []
        while self.n_issued < len(self.plan) and (self.n_issued - NSLOT) in self.rel_ev:
            self._issue(self.n_issued)
            self.n_issued += 1


class Plan:
    def __init__(self):
        self.keys = []
        self.uniq = {}

    def add(self, key):
        self.keys.append(key)
        if key[0] != "state" and key not in self.uniq:
            self.uniq[key] = len(self.uniq)


def ffn_plan(P, i, j):
    for g in range(NG + 1):
        for hf in range(2):
            if g < NG:
                P.add(("gate", i, j, g, hf))
                P.add(("up", i, j, g, hf))
            if g > 0:
                P.add(("down", i, j, g - 1, hf))


def materialize(key, inp):
    k = key[0]
    if k in ("gate", "up"):
        _, i, j, g, hf = key
        W = inp["w_ffn_gate" if k == "gate" else "w_ffn_up"][i, j]
        c0 = g * 512 + hf * 256
        return W[:, c0:c0 + 256].reshape(16, 128, 256).transpose(1, 0, 2).reshape(128, UE)
    if k == "down":
        _, i, j, g, hf = key
        W = inp["w_ffn_down"][i, j]
        return W[g * 512:(g + 1) * 512, hf * 1024:(hf + 1) * 1024].reshape(4, 128, 1024).transpose(1, 0, 2).reshape(128, UE)
    if k == "cols":
        _, name, c0 = key
        W = inp[name][0]
        return W[:, c0:c0 + 256].reshape(16, 128, 256).transpose(1, 0, 2).reshape(128, UE)
    if k == "pw1":
        _, c = key
        W = inp["conv_w_pw1"][0]
        a = np.stack([W[:, c * 128:(c + 1) * 128], W[:, 2048 + c * 128:2048 + (c + 1) * 128]], axis=1)
        return a.reshape(16, 128, 2, 128).transpose(1, 0, 2, 3).reshape(128, UE)
    raise KeyError(key)


NBC = 352
NBG = 288
V_B1V, V_B1G, V_BDW, V_LNG, V_LNB, V_B2, V_GN, V_FLAG, V_GMT = 112, 128, 144, 160, 176, 192, 208, 212, 213
C_ID, C_U, C_M, C_US, C_MS, C_SEQ, C_CM, C_GM = 0, 128, 256, 384, 448, 512, 528, 600


def build_program(plan_keys=None):
    record = plan_keys is None
    nc = bass.Bass("TRN2", target_bir_lowering=False)
    rec_keys = []
    uniq = {}
    if not record:
        for k in plan_keys:
            if k[0] != "state" and k not in uniq:
                uniq[k] = len(uniq)
    NU = max(1, len(uniq))
    x_in = nc.dram_tensor("x_in", [T, D], F32, kind="ExternalInput").ap()
    vecs = nc.dram_tensor("vecs", [128, 256], F32, kind="ExternalInput").ap()
    cst = nc.dram_tensor("cst", [128, 1024], F32, kind="ExternalInput").ap()
    wdw_d = nc.dram_tensor("wdw", [128, 16 * 31], F32, kind="ExternalInput").ap()
    wg1_d = nc.dram_tensor("wg1", [128, 16 * 16], F32, kind="ExternalInput").ap()
    wg2_d = nc.dram_tensor("wg2a", [32, 1024], F32, kind="ExternalInput").ap()
    sconv = nc.dram_tensor("sconv", [16, 30, D], F32, kind="ExternalInput").ap()
    sgla = nc.dram_tensor("sgla", [16, 4, 256, 512], F32, kind="ExternalInput").ap()
    wpack = nc.dram_tensor("wpack", [NU, 128, UE], F32, kind="ExternalInput").ap()
    y_out = nc.dram_tensor("y_out", [T, D], F32, kind="ExternalOutput").ap()
    ncp_out = nc.dram_tensor("ncp_out", [30, D], F32, kind="ExternalOutput").ap()
    ngp_out = nc.dram_tensor("ngp_out", [1024, 512], F32, kind="ExternalOutput").ap()
    ncs_out = nc.dram_tensor("ncs_out", [16, 30, D], F32, kind="ExternalOutput").ap()
    ngs_out = nc.dram_tensor("ngs_out", [16, 4, 256, 512], F32, kind="ExternalOutput").ap()
    mix_keys = []
    if not record:
        for k in plan_keys:
            if k[0] in ("pw1", "cols") and k not in mix_keys:
                mix_keys.append(k)
    mix_idx = {k: i for i, k in enumerate(mix_keys)}
    wbf = nc.dram_tensor("wbf", [max(1, len(mix_keys)), 128, UE], BF16)
    cc_src = nc.dram_tensor("cc_src", [1024, 512], F32)
    cc_dst = nc.dram_tensor("cc_dst", [2048, 512], F32)

    S = Sched(nc)
    S._newsem("cc")
    ctx = []

    alloc_id = [0]

    def alloc(name, shape, dt):
        alloc_id[0] += 1
        cm = nc.sbuf_tensor("%s_%d" % (name, alloc_id[0]), shape, dt)
        t = cm.__enter__()
        ctx.append(cm)
        return t

    def free(n):
        for _ in range(n):
            ctx.pop().__exit__(None, None, None)

    x = alloc("x", [128, KC, T], F32)
    xb = [bufs(len(SEGS)) for _ in range(KC)]
    ring_t = alloc("ring", [128, NSLOT, UE], BF16)
    vec_t = alloc("vec_t", [128, 256], F32)
    cst_t = alloc("cst_t", [128, 1024], F32)
    ones_b = alloc("ones_b", [128, 128], BF16)
    b_vec, b_cst, b_ones = Buf(), Buf(), Buf()
    pcm = nc.psum_tensor("ps", [128, 8, 512], F32)
    ps = pcm.__enter__()
    psb = bufs(8)
    bank_rr = [0]
    ALLB = (0, 1, 2, 3, 4, 5, 6, 7)
    POOL6 = (0, 1, 2, 3, 4, 5)
    POOL5 = (0, 1, 2, 3, 4)

    def bank(pool=ALLB):
        i = pool[bank_rr[0] % len(pool)]
        bank_rr[0] += 1
        return i

    ring = Ring(S, ring_t)
    if not record:
        plan_aps = []
        scratch_flags = []
        for key in plan_keys:
            scratch_flags.append(key in mix_idx)
            if key in mix_idx:
                plan_aps.append([(wbf.ap()[mix_idx[key], :, :], lambda tile, slot: tile[:, slot, :])])
            elif key[0] == "state":
                _, gq, hd = key
                lst = []
                for s4 in range(4):
                    src = sgla[gq * 4 + s4, hd, :, :].rearrange("(c p) v -> p c v", p=128)
                    lst.append((src, lambda tile, slot, s4=s4: tile[:, slot, s4 * 1024:(s4 + 1) * 1024].rearrange(
                        "p (c v) -> p c v", c=2)))
                plan_aps.append(lst)
            else:
                plan_aps.append([(wpack[uniq[key], :, :], lambda tile, slot: tile[:, slot, :])])
        conv_jobs = [(wbf.ap()[mix_idx[k], :, :], wpack[uniq[k], :, :]) for k in mix_keys]
        ring.set_plan(plan_aps, scratch_flags, conv_jobs)
        ring.prime()
    rec_state = {"n": 0}

    def acquire(key):
        if record:
            rec_keys.append(key)
            u = rec_state["n"]
            rec_state["n"] += 1
            return u % NSLOT
        assert plan_keys[ring.n_acq] == key, (ring.n_acq, plan_keys[ring.n_acq], key)
        _, slot = ring.acquire()
        return slot

    def release():
        if not record:
            ring.release_all()

    ident = cst_t[:, C_ID:C_ID + 128]
    S.dma("sp", vec_t[:], vecs[:, :], writes=[b_vec])
    S.dma("sp", cst_t[:], cst[:, :], writes=[b_cst])
    S.op("dve", lambda e: e.memset(ones_b[:], 1.0), writes=[b_ones])

    def vcol(c):
        return vec_t[:, c:c + 1]

    TT = [(i * 128, 128) for i in range(8)] + [(1024, TP + TS)]
    tokin = [alloc("tokin%d" % i, [128, D], F32) for i in range(4)]
    tokb = bufs(4)
    for ti, (r0, nt) in enumerate(TT):
        tk, tb = tokin[ti % 4], tokb[ti % 4]
        S.dma("sp", tk[:nt, :], x_in[r0:r0 + nt, :], writes=[tb])
        segs = segs_of(r0, nt)
        for k4 in range(4):
            bi = bank()
            fns = []
            for j in range(4):
                kc = k4 * 4 + j
                fns.append(lambda e, kc=kc, j=j, bi=bi, tk=tk, nt=nt: e.transpose(
                    ps[:, bi, j * 128:j * 128 + nt], tk[:nt, kc * 128:(kc + 1) * 128], ident[:nt, :nt]))
            S.mm(fns, reads=[tb, b_cst], writes=[psb[bi]])
            wr = [xb[k4 * 4 + j][s] for j in range(4) for s in segs]
            if k4 % 2:
                S.op("act", lambda e, bi=bi, k4=k4, r0=r0, nt=nt: e.activation(
                    out=x[:, k4 * 4:k4 * 4 + 4, r0:r0 + nt],
                    in_=ps[:, bi, :].rearrange("p (j t) -> p j t", j=4)[:, :, :nt], func=AF.Copy),
                    reads=[psb[bi]], writes=wr)
            else:
                S.op("dve", lambda e, bi=bi, k4=k4, r0=r0, nt=nt: e.tensor_copy(
                    out=x[:, k4 * 4:k4 * 4 + 4, r0:r0 + nt],
                    in_=ps[:, bi, :].rearrange("p (j t) -> p j t", j=4)[:, :, :nt]),
                    reads=[psb[bi]], writes=wr)
    S.barrier()
    free(4)

    def rms_stats_full(sqt, sqb, rt, rtb):
        sbanks = [5, 6, 7]
        for kc in range(KC):
            q, qb = sqt[kc % 2], sqb[kc % 2]
            S.op("act", lambda e, kc=kc, q=q: e.activation(out=q[:], in_=x[:, kc, :], func=AF.Square),
                 reads=xb[kc], writes=[qb])
            fns = []
            for bl, (c0, n, _) in enumerate(BLKS):
                fns.append(lambda e, bl=bl, c0=c0, n=n, q=q, kc=kc: e.matmul(
                    ps[:, sbanks[bl], :n], lhsT=ones_b[:], rhs=q[:, c0:c0 + n], start=(kc == 0), stop=(kc == KC - 1)))
            S.mm(fns, reads=[qb, b_ones], writes=[psb[b] for b in sbanks])
        for bl, (c0, n, _) in enumerate(BLKS):
            S.op("act", lambda e, bl=bl, c0=c0, n=n: e.activation(
                out=rt[:, c0:c0 + n], in_=ps[:, sbanks[bl], :n], func=AF.Sqrt, scale=1.0 / D, bias=EPS),
                reads=[psb[sbanks[bl]]], writes=[rtb])
        S.op("dve", lambda e: e.reciprocal(out=rt[:], in_=rt[:]), reads=[rtb], writes=[rtb])

    def rms_block(c0, n, segs, gcol, h, hb, sq, sqb, rt, rtb):
        xs = lambda kc: [xb[kc][s] for s in segs]
        for kc in range(KC):
            q, qb = sq[kc % 2], sqb[kc % 2]
            S.op("act", lambda e, kc=kc, q=q: e.activation(out=q[:, :n], in_=x[:, kc, c0:c0 + n], func=AF.Square),
                 reads=xs(kc), writes=[qb])
            S.mm([lambda e, kc=kc, q=q: e.matmul(ps[:, 7, :n], lhsT=ones_b[:], rhs=q[:, :n],
                                                 start=(kc == 0), stop=(kc == KC - 1))],
                 reads=[qb, b_ones], writes=[psb[7]])
        S.op("act", lambda e: e.activation(out=rt[:, :n], in_=ps[:, 7, :n], func=AF.Sqrt, scale=1.0 / D, bias=EPS),
             reads=[psb[7]], writes=[rtb])
        S.op("dve", lambda e: e.reciprocal(out=rt[:, :n], in_=rt[:, :n]), reads=[rtb], writes=[rtb])
        for kc in range(KC):
            S.op("dve", lambda e, kc=kc: e.scalar_tensor_tensor(
                out=h[:, kc, :n], in0=x[:, kc, c0:c0 + n], scalar=vcol(gcol + kc), in1=rt[:, :n],
                op0=ALU.mult, op1=ALU.mult),
                reads=xs(kc) + [rtb, b_vec], writes=[hb[kc]])

    def ffn(i, j, blks=BLKS):
        h = alloc("h", [128, KC, T], BF16)
        hb = bufs(KC)
        sqt = [alloc("sq%d" % a, [128, T], BF16) for a in range(2)]
        sqb = bufs(2)
        rt = alloc("rt", [128, T], F32)
        rtb = Buf()
        hid = [alloc("hid%d" % a, [128, 4, T], BF16) for a in range(2)]
        hidb = [bufs(4), bufs(4)]
        sg = [alloc("sg%d" % a, [128, 512], BF16) for a in range(3)]
        sgb = bufs(3)
        sgi = [0]
        gcol = (i * 2 + j) * 16
        rms_stats_full(sqt, sqb, rt, rtb)
        for kc in range(KC):
            S.op("dve", lambda e, kc=kc: e.scalar_tensor_tensor(
                out=h[:, kc, :], in0=x[:, kc, :], scalar=vcol(gcol + kc), in1=rt[:],
                op0=ALU.mult, op1=ALU.mult),
                reads=xb[kc] + [rtb, b_vec], writes=[hb[kc]])
        for g in range(NG + 1):
            for hf in range(2):
                if g < NG:
                    sg_slot = acquire(("gate", i, j, g, hf))
                    su_slot = acquire(("up", i, j, g, hf))
                if g > 0:
                    sd_slot = acquire(("down", i, j, g - 1, hf))
                for mm_ in range(2):
                    m = hf * 2 + mm_
                    if g < NG:
                        hcur, hcb = hid[g % 2], hidb[g % 2]
                        for bl, (c0, n, _) in enumerate(blks):
                            bg = bank()
                            fns = [lambda e, kc=kc, bg=bg, c0=c0, n=n, sl=sg_slot, mm_=mm_: e.matmul(
                                ps[:, bg, :n], lhsT=ring_t[:, sl, kc * 256 + mm_ * 128:kc * 256 + mm_ * 128 + 128],
                                rhs=h[:, kc, c0:c0 + n], start=(kc == 0), stop=(kc == KC - 1)) for kc in range(KC)]
                            S.mm(fns, reads=hb, writes=[psb[bg]])
                            si = sgi[0] % 3
                            sgi[0] += 1
                            S.op("act", lambda e, bg=bg, n=n, si=si: e.activation(
                                out=sg[si][:, :n], in_=ps[:, bg, :n], func=AF.Silu),
                                reads=[psb[bg]], writes=[sgb[si]])
                            bu = bank()
                            fns = [lambda e, kc=kc, bu=bu, c0=c0, n=n, sl=su_slot, mm_=mm_: e.matmul(
                                ps[:, bu, :n], lhsT=ring_t[:, sl, kc * 256 + mm_ * 128:kc * 256 + mm_ * 128 + 128],
                                rhs=h[:, kc, c0:c0 + n], start=(kc == 0), stop=(kc == KC - 1)) for kc in range(KC)]
                            S.mm(fns, reads=hb, writes=[psb[bu]])
                            S.op("dve", lambda e, bu=bu, n=n, si=si, c0=c0, m=m, hcur=hcur: e.tensor_tensor(
                                out=hcur[:, m, c0:c0 + n], in0=ps[:, bu, :n], in1=sg[si][:, :n], op=ALU.mult),
                                reads=[psb[bu], sgb[si]], writes=[hcb[m]])
                    if g > 0:
                        hprev, hpb = hid[(g - 1) % 2], hidb[(g - 1) % 2]
                        for o4 in range(4):
                            oc = hf * 8 + mm_ * 4 + o4
                            ol = mm_ * 4 + o4
                            for bl, (c0, n, segs) in enumerate(blks):
                                bd = bank()
                                fns = [lambda e, hc=hc, bd=bd, c0=c0, n=n, sl=sd_slot, ol=ol, hprev=hprev: e.matmul(
                                    ps[:, bd, :n], lhsT=ring_t[:, sl, hc * 1024 + ol * 128:hc * 1024 + ol * 128 + 128],
                                    rhs=hprev[:, hc, c0:c0 + n], start=(hc == 0), stop=(hc == 3)) for hc in range(4)]
                                S.mm(fns, reads=hpb, writes=[psb[bd]])
                                xw = [xb[oc][s] for s in segs]
                                S.op("dve", lambda e, bd=bd, n=n, c0=c0, oc=oc: e.scalar_tensor_tensor(
                                    out=x[:, oc, c0:c0 + n], in0=ps[:, bd, :n], scalar=0.5, in1=x[:, oc, c0:c0 + n],
                                    op0=ALU.mult, op1=ALU.add),
                                    reads=[psb[bd]] + xw, writes=xw)
                release()
        S.barrier()
        free(9)

    def conv_mixer():
        conv_phase(False)
        conv_phase(True)

    def conv_phase(samp_phase):
        n0 = len(ctx)
        nb = TS if samp_phase else NBC
        wdw_t = alloc("wdw_t", [128, 16, 31], F32)
        b_wdw = Buf()
        S.dma("sp", wdw_t[:].rearrange("p c j -> p (c j)"), wdw_d[:, :], writes=[b_wdw])
        hbk = alloc("c_h", [128, KC, nb], BF16)
        hbb = bufs(KC)
        sq = [alloc("c_sq%d" % a, [128, nb], BF16) for a in range(2)]
        sqb = bufs(2)
        rt = alloc("c_rt", [128, nb], F32)
        rtb = Buf()
        sig = [alloc("c_sig%d" % a, [128, nb], F32) for a in range(2)]
        sigb = bufs(2)
        dg = [alloc("c_dg%d" % a, [128, 31, 128], BF16) for a in range(2)]
        dgb = bufs(2)
        y = alloc("c_y", [128, KC, nb], F32)
        yb = bufs(KC)
        ybf = [alloc("c_ybf%d" % a, [128, nb], BF16) for a in range(2)]
        ybfb = bufs(2)
        ysq = [alloc("c_ysq%d" % a, [128, nb], BF16) for a in range(2)]
        ysqb = bufs(2)
        mean = alloc("c_mean", [128, nb], F32)
        rstd = alloc("c_rstd", [128, nb], F32)
        mr = alloc("c_mr", [128, nb], F32)
        b_mean, b_rstd, b_mr = Buf(), Buf(), Buf()
        t1 = [alloc("c_t1%d" % a, [128, nb], F32) for a in range(2)]
        t1b = bufs(2)
        uf = alloc("c_uf", [128, KC, 64], F32)
        ufb = bufs(KC)
        if samp_phase:
            tko = alloc("c_tko", [64, D], F32)
            tko_alias = []
        else:
            tko = y[:64, 0:6, :].rearrange("p a b -> p (a b)")[:, 0:D]
            tko_alias = yb[0:6]
        tkob = Buf()

        def run_block(c0, n, segs, samp, npre, ext_w, ext_r, extb, last_main):
            rms_block(c0, n, segs, 64, hbk, hbb, sq, sqb, rt, rtb)

            def pw1(c):
                slot = acquire(("pw1", c))
                bv, bg_ = bank(POOL5), bank(POOL5)
                for half, bb in ((0, bv), (1, bg_)):
                    fns = [lambda e, kc=kc, bb=bb, half=half, slot=slot: e.matmul(
                        ps[:, bb, :n], lhsT=ring_t[:, slot, kc * 256 + half * 128:kc * 256 + half * 128 + 128],
                        rhs=hbk[:, kc, :n], start=(kc == 0), stop=(kc == KC - 1)) for kc in range(KC)]
                    S.mm(fns, reads=hbb, writes=[psb[bb]])
                release()
                si = c % 2
                S.op("act", lambda e: e.activation(out=sig[si][:, :n], in_=ps[:, bg_, :n], func=AF.Sigmoid,
                                                   bias=vcol(V_B1G + c)),
                     reads=[psb[bg_], b_vec], writes=[sigb[si]])
                if npre:
                    S.op("dve", lambda e: e.tensor_tensor(out=sig[si][:, :npre], in0=sig[si][:, :npre],
                                                          in1=cst_t[:, C_CM:C_CM + npre], op=ALU.mult),
                         reads=[sigb[si], b_cst], writes=[sigb[si]])
                S.op("dve", lambda e: e.scalar_tensor_tensor(
                    out=ext_w(c), in0=(ps[:, bv, :n].rearrange("p (s t) -> p s t", t=4) if samp else ps[:, bv, :n]),
                    scalar=vcol(V_B1V + c),
                    in1=(sig[si][:, :n].rearrange("p (s t) -> p s t", t=4) if samp else sig[si][:, :n]),
                    op0=ALU.add, op1=ALU.mult),
                    reads=[psb[bv], sigb[si], b_vec], writes=[extb[c]])
                if last_main or samp:
                    k0 = 0 if samp else n - 30
                    w = n - k0
                    S.op("dve", lambda e: e.scalar_tensor_tensor(
                        out=uf[:, c, :w], in0=ps[:, bv, k0:n], scalar=vcol(V_B1V + c), in1=sig[si][:, k0:n],
                        op0=ALU.add, op1=ALU.mult),
                        reads=[psb[bv], sigb[si], b_vec], writes=[ufb[c]])
                di = c % 2
                S.op("dve", lambda e: e.tensor_tensor(
                    out=dg[di][:], in0=ident.unsqueeze(1).to_broadcast([128, 31, 128]),
                    in1=wdw_t[:, c, :].unsqueeze(2).to_broadcast([128, 31, 128]), op=ALU.mult),
                    reads=[b_cst, b_wdw], writes=[dgb[di]])

            def dconv(c):
                di = c % 2
                by = bank(POOL5)
                fns = [lambda e, j=j: e.matmul(ps[:, by, :n], lhsT=dg[di][:, j, :], rhs=ext_r(c, j),
                                               start=(j == 0), stop=(j == 30)) for j in range(31)]
                S.mm(fns, reads=[dgb[di], extb[c]], writes=[psb[by]])
                S.op("act", lambda e: e.activation(out=y[:, c, :n], in_=ps[:, by, :n], func=AF.Identity,
                                                   bias=vcol(V_BDW + c)),
                     reads=[psb[by], b_vec], writes=[yb[c]])
                yi = c % 2
                S.op("act", lambda e: e.activation(out=ybf[yi][:, :n], in_=ps[:, by, :n], func=AF.Identity,
                                                   bias=vcol(V_BDW + c)),
                     reads=[psb[by], b_vec], writes=[ybfb[yi]])
                S.op("act", lambda e: e.activation(out=ysq[yi][:, :n], in_=ps[:, by, :n], func=AF.Square,
                                                   bias=vcol(V_BDW + c)),
                     reads=[psb[by], b_vec], writes=[ysqb[yi]])

            def stats(c):
                di = c % 2
                S.mm([lambda e: e.matmul(ps[:, 6, :n], lhsT=ones_b[:], rhs=ybf[di][:, :n],
                                         start=(c == 0), stop=(c == KC - 1)),
                      lambda e: e.matmul(ps[:, 5, :n], lhsT=ones_b[:], rhs=ysq[di][:, :n],
                                         start=(c == 0), stop=(c == KC - 1))],
                     reads=[ybfb[di], ysqb[di], b_ones], writes=[psb[6], psb[5]])

            for c in range(KC + 2):
                if c < KC:
                    pw1(c)
                if 1 <= c <= KC:
                    dconv(c - 1)
                if c >= 2:
                    stats(c - 2)
            S.op("act", lambda e: e.activation(out=mean[:, :n], in_=ps[:, 6, :n], func=AF.Identity, scale=1.0 / D),
                 reads=[psb[6]], writes=[b_mean])
            S.op("dve", lambda e: e.tensor_tensor(out=mr[:, :n], in0=mean[:, :n], in1=mean[:, :n], op=ALU.mult),
                 reads=[b_mean], writes=[b_mr])
            S.op("dve", lambda e: e.scalar_tensor_tensor(out=rstd[:, :n], in0=ps[:, 5, :n], scalar=1.0 / D,
                                                         in1=mr[:, :n], op0=ALU.mult, op1=ALU.subtract),
                 reads=[psb[5], b_mr], writes=[b_rstd])
            S.op("act", lambda e: e.activation(out=rstd[:, :n], in_=rstd[:, :n], func=AF.Sqrt, bias=EPS),
                 reads=[b_rstd], writes=[b_rstd])
            S.op("dve", lambda e: e.reciprocal(out=rstd[:, :n], in_=rstd[:, :n]), reads=[b_rstd], writes=[b_rstd])
            S.op("dve", lambda e: e.tensor_tensor(out=mr[:, :n], in0=mean[:, :n], in1=rstd[:, :n], op=ALU.mult),
                 reads=[b_mean, b_rstd], writes=[b_mr])
            for c in range(KC):
                ti = c % 2
                S.op("dve", lambda e: e.tensor_tensor(out=t1[ti][:, :n], in0=y[:, c, :n], in1=rstd[:, :n], op=ALU.mult),
                     reads=[yb[c], b_rstd], writes=[t1b[ti]])
                S.op("dve", lambda e: e.tensor_tensor(out=t1[ti][:, :n], in0=t1[ti][:, :n], in1=mr[:, :n],
                                                      op=ALU.subtract),
                     reads=[t1b[ti], b_mr], writes=[t1b[ti]])
                S.op("act", lambda e: e.activation(out=hbk[:, c, :n], in_=t1[ti][:, :n], func=AF.Silu,
                                                   scale=vcol(V_LNG + c), bias=vcol(V_LNB + c)),
                     reads=[t1b[ti], b_vec], writes=[hbb[c]])
            for o2 in range(8):
                slot = acquire(("cols", "conv_w_pw2", o2 * 256))
                for oo in range(2):
                    oc = o2 * 2 + oo
                    bo = bank(POOL5)
                    fns = [lambda e, kc=kc: e.matmul(
                        ps[:, bo, :n], lhsT=ring_t[:, slot, kc * 256 + oo * 128:kc * 256 + oo * 128 + 128],
                        rhs=hbk[:, kc, :n], start=(kc == 0), stop=(kc == KC - 1)) for kc in range(KC)]
                    S.mm(fns, reads=hbb, writes=[psb[bo]])
                    xw = [xb[oc][s] for s in segs]
                    S.op("dve", lambda e: e.scalar_tensor_tensor(
                        out=x[:, oc, c0:c0 + n], in0=ps[:, bo, :n], scalar=vcol(V_B2 + oc), in1=x[:, oc, c0:c0 + n],
                        op0=ALU.add, op1=ALU.add),
                        reads=[psb[bo], b_vec] + xw, writes=xw)
                release()

        def out_state(w, dst_fn):
            for k4 in range(4):
                bi = bank(POOL5)
                fns = [lambda e, j=j: e.transpose(ps[:w, bi, j * 128:(j + 1) * 128], uf[:, k4 * 4 + j, :w], ident[:, :])
                       for j in range(4)]
                S.mm(fns, reads=[ufb[k4 * 4 + j] for j in range(4)] + [b_cst], writes=[psb[bi]])
                S.op("act", lambda e: e.activation(out=tko[:w, k4 * 512:(k4 + 1) * 512], in_=ps[:w, bi, :], func=AF.Copy),
                     reads=[psb[bi]], writes=[tkob] + tko_alias)
            dst_fn()

        if not samp_phase:
            ext = alloc("c_ext", [128, KC, 30 + NBC], BF16)
            extb = bufs(KC)
            S.op("dve", lambda e: e.memset(ext[:, :, 0:30], 0.0), writes=extb)
            for bi_, (c0, n, npre) in enumerate(CONV_BLOCKS):
                last_main = bi_ == len(CONV_BLOCKS) - 1
                run_block(c0, n, segs_of(c0, n), False, npre, lambda c: ext[:, c, 30:30 + n], lambda c, j: ext[:, c, j:j + n],
                          extb, last_main)
                if not last_main:
                    S.op("act", lambda e: e.activation(out=ext[:, :, 0:30], in_=ext[:, :, n:n + 30], func=AF.Copy),
                         reads=extb, writes=extb)
            out_state(30, lambda: S.dma("sp", ncp_out[:, :], tko[:30, :], reads=[tkob], is_output=True))
        else:
            exs = alloc("c_exs", [128, KC, 16, 34], BF16)
            exsb = bufs(KC)
            stk = [alloc("c_stk%d" % a, [120, D], F32) for a in range(2)]
            stkb = bufs(2)
            for gq in range(4):
                tk, tb = stk[gq % 2], stkb[gq % 2]
                S.dma("sp", tk[:, :], sconv[gq * 4:(gq + 1) * 4, :, :].rearrange("s j d -> (s j) d"), writes=[tb])
                for k4 in range(4):
                    bi = bank(POOL5)
                    fns = [lambda e, j=j: e.transpose(ps[:, bi, j * 120:(j + 1) * 120],
                                                      tk[:, (k4 * 4 + j) * 128:(k4 * 4 + j + 1) * 128], ident[:120, :120])
                           for j in range(4)]
                    S.mm(fns, reads=[tb, b_cst], writes=[psb[bi]])
                    for j in range(4):
                        c = k4 * 4 + j
                        S.op("act", lambda e: e.activation(
                            out=exs[:, c, gq * 4:(gq + 1) * 4, 0:30],
                            in_=ps[:, bi, j * 120:(j + 1) * 120].rearrange("p (s t) -> p s t", s=4), func=AF.Copy),
                            reads=[psb[bi]], writes=[exsb[c]])
            c0, n = CS, TS
            run_block(c0, n, segs_of(c0, n), True, 0, lambda c: exs[:, c, :, 30:34], lambda c, j: exs[:, c, :, j:j + 4], exsb, False)

            def samp_dst():
                for s_ in range(16):
                    S.dma("sp", ncs_out[s_, 26:30, :], tko[s_ * 4:(s_ + 1) * 4, :], reads=[tkob], is_output=True)
            out_state(64, samp_dst)
        S.barrier()
        free(len(ctx) - n0)

    def gla_mixer():
        Sst = alloc("g_S", [128, 8, 512], F32)
        Sstb = bufs(8)
        gla_phase("A", Sst, Sstb)
        gla_phase("S", Sst, Sstb)
        gla_phase("B", Sst, Sstb)
        free(1)

    def gla_phase(mode, Sst, Sstb):
        samp_phase = mode == "S"
        n0 = len(ctx)
        nb = TS if samp_phase else NBG
        nch = 1 if samp_phase else 3
        gh = alloc("g_h", [128, KC, nb], BF16)
        ghb = bufs(KC)
        sq = [alloc("g_sq%d" % a, [128, nb], BF16) for a in range(2)]
        sqb = bufs(2)
        rt = alloc("g_rt", [128, nb], F32)
        rtb = Buf()
        g1a = alloc("g_g1a", [32, nb], BF16)
        b_g1a = Buf()
        wg1 = alloc("g_wg1", [128, 16, 16], BF16)
        wg2 = alloc("g_wg2", [32, 1024], BF16)
        b_wg1, b_wg2 = Buf(), Buf()
        la = alloc("g_la", [128, nch, 1024], F32)
        lab = bufs(3)
        E = alloc("g_E", [128, nch, 256], F32)
        Eb = bufs(3)
        kh = alloc("g_kh", [128, max(2, nch), 256], BF16)
        khb = bufs(3)
        vt = alloc("g_vt", [128, nch, 512], BF16)
        vtb = bufs(3)
        ep = alloc("g_ep", [128, 2, nb], F32)
        en = alloc("g_en", [128, 2, nb], F32)
        epb, enb = bufs(2), bufs(2)
        qt = alloc("g_qt", [128, 2, nb], BF16)
        kt = alloc("g_kt", [128, 2, nb], BF16)
        qtb, ktb = bufs(2), bufs(2)
        sc = alloc("g_sc", [128, nch, 128], BF16)
        scb = bufs(3)
        oT = alloc("g_oT", [128, 4, nb], F32)
        oTb = Buf()
        osq = [alloc("g_osq%d" % a, [128, nb], BF16) for a in range(4)]
        osqb = bufs(4)
        grt = alloc("g_grt", [128, nb], F32)
        b_grt = Buf()
        rr = alloc("g_r", [128, 4, nb], BF16)
        rrb = bufs(4)
        og = alloc("g_og", [128, KC, nb], BF16)
        ogb = bufs(KC)
        t1 = [alloc("g_t1%d" % a, [128, nb], F32) for a in range(2)]
        t1b = bufs(2)
        if samp_phase:
            s0f = [alloc("g_s0f%d" % a, [128, 2, 512], F32) for a in range(4)]
            s0fb = bufs(4)
            snw = [alloc("g_snw%d" % a, [128, 2, 512], F32) for a in range(4)]
            snwb = bufs(4)
            Sbs = [alloc("g_Sbs%d" % a, [128, 2, 512], BF16) for a in range(2)]
            Sbsb = bufs(2)
            Sb = None
            Sbb = []
        else:
            Sb = alloc("g_Sb", [128, 2, 512], BF16)
            Sbb = bufs(2)
            if mode == "A":
                S.op("dve", lambda e: e.memset(Sst[:], 0.0), writes=Sstb)

        S.dma("pool", wg1[:].rearrange("p a b -> p (a b)"), wg1_d[:, :], writes=[b_wg1])
        S.dma("pool", wg2[:], wg2_d[:, :], writes=[b_wg2])
        S.op("dve", lambda e: e.memset(g1a[:], 1.0), writes=[b_g1a])

        def gla_block(c0, n, segs, samp, chunks3, phaseB, part="all"):
            chunks = [(o, nt) for (o, nt, _) in chunks3]
            cpre = [m for (_, _, m) in chunks3]
            GP = POOL5 if samp else POOL6
            Um = cst_t[:, C_US:C_US + 64] if samp else cst_t[:, C_U:C_U + 128]
            Mm = cst_t[:, C_MS:C_MS + 64] if samp else cst_t[:, C_M:C_M + 128]
            if part in ("all", "pro"):
                rms_block(c0, n, segs, 80, gh, ghb, sq, sqb, rt, rtb)
                bq = bank(GP)
                S.mm([lambda e, kc=kc: e.matmul(ps[:16, bq, :n], lhsT=wg1[:, kc, :], rhs=gh[:, kc, :n],
                                                start=(kc == 0), stop=(kc == KC - 1)) for kc in range(KC)],
                     reads=ghb + [b_wg1], writes=[psb[bq]])
                S.op("act", lambda e: e.activation(out=g1a[0:16, :n], in_=ps[:16, bq, :n], func=AF.Copy),
                     reads=[psb[bq]], writes=[b_g1a])
                for ci, (o, nt) in enumerate(chunks):
                    for half in range(2):
                        bz = bank(GP)
                        S.mm([lambda e: e.matmul(ps[:nt, bz, :512], lhsT=g1a[0:17, o:o + nt],
                                                 rhs=wg2[0:17, half * 512:(half + 1) * 512], start=True, stop=True)],
                             reads=[b_g1a, b_wg2], writes=[psb[bz]])
                        dst = la[:nt, ci, half * 512:(half + 1) * 512]
                        S.op("act", lambda e: e.activation(out=dst, in_=ps[:nt, bz, :512], func=AF.Exp, scale=-1.0),
                             reads=[psb[bz]], writes=[lab[ci]])
                        S.op("act", lambda e: e.activation(out=dst, in_=dst, func=AF.Ln, bias=1.0),
                             reads=[lab[ci]], writes=[lab[ci]])
                        if cpre[ci]:
                            S.op("dve", lambda e: e.tensor_scalar(out=dst, in0=dst, scalar1=-1.0 / 16, scalar2=vec_t[:nt, V_GMT:V_GMT + 1],
                                                                  op0=ALU.mult, op1=ALU.mult),
                                 reads=[lab[ci], b_vec], writes=[lab[ci]])
                        else:
                            S.op("dve", lambda e: e.tensor_scalar(out=dst, in0=dst, scalar1=-1.0 / 16, scalar2=None,
                                                                  op0=ALU.mult),
                                 reads=[lab[ci]], writes=[lab[ci]])
            if part == "pro":
                return
            for hd in (range(4) if part in ("all", "heads") else ()):
                if phaseB and not samp:
                    for dc in range(2):
                        S.op("act", lambda e: e.activation(out=Sb[:, dc, :], in_=Sst[:, hd * 2 + dc, :], func=AF.Copy),
                             reads=[Sstb[hd * 2 + dc]], writes=[Sbb[dc]])
                for ci, (o, nt) in enumerate(chunks):
                    be = bank(GP)
                    S.mm([lambda e: e.matmul(ps[:nt, be, :256], lhsT=Mm[:nt, :nt], rhs=la[:nt, ci, hd * 256:(hd + 1) * 256],
                                             start=True, stop=True)],
                         reads=[lab[ci], b_cst], writes=[psb[be]])
                    S.op("act", lambda e: e.activation(out=E[:nt, ci, :], in_=ps[:nt, be, :256], func=AF.Exp),
                         reads=[psb[be]], writes=[Eb[ci]])
                    if cpre[ci]:
                        S.op("dve", lambda e: e.tensor_scalar(out=E[:nt, ci, :], in0=E[:nt, ci, :],
                                                              scalar1=vec_t[:nt, V_GMT:V_GMT + 1], scalar2=None, op0=ALU.mult),
                             reads=[Eb[ci], b_vec], writes=[Eb[ci]])
                    for dc in range(2):
                        bd_ = bank(GP)
                        S.mm([lambda e: e.matmul(ps[:, bd_, :nt], lhsT=la[:nt, ci, hd * 256 + dc * 128:hd * 256 + dc * 128 + 128],
                                                 rhs=Um[:nt, :nt], start=True, stop=True)],
                             reads=[lab[ci], b_cst], writes=[psb[bd_]])
                        S.op("act", lambda e: e.activation(out=ep[:, dc, o:o + nt], in_=ps[:, bd_, :nt], func=AF.Exp),
                             reads=[psb[bd_]], writes=[epb[dc]])
                        if phaseB:
                            S.op("act", lambda e: e.activation(out=en[:, dc, o:o + nt], in_=ps[:, bd_, :nt], func=AF.Exp,
                                                               scale=-1.0),
                                 reads=[psb[bd_]], writes=[enb[dc]])
                sk = acquire(("cols", "gla_w_k", hd * 256))
                for ci, (o, nt) in enumerate(chunks):
                    bk = bank(GP)
                    S.mm([lambda e, kc=kc: e.matmul(ps[:nt, bk, :256], lhsT=gh[:, kc, o:o + nt],
                                                    rhs=ring_t[:, sk, kc * 256:(kc + 1) * 256],
                                                    start=(kc == 0), stop=(kc == KC - 1)) for kc in range(KC)],
                         reads=ghb, writes=[psb[bk]])
                    S.op("dve", lambda e: e.tensor_tensor(out=kh[:nt, ci, :], in0=ps[:nt, bk, :256], in1=E[:nt, ci, :],
                                                          op=ALU.mult),
                         reads=[psb[bk], Eb[ci]], writes=[khb[ci]])
                if phaseB:
                    for dc in range(2):
                        bk = bank(GP)
                        S.mm([lambda e, kc=kc: e.matmul(ps[:, bk, :n], lhsT=ring_t[:, sk, kc * 256 + dc * 128:kc * 256 + dc * 128 + 128],
                                                        rhs=gh[:, kc, :n], start=(kc == 0), stop=(kc == KC - 1))
                              for kc in range(KC)],
                             reads=ghb, writes=[psb[bk]])
                        S.op("dve", lambda e: e.tensor_tensor(out=kt[:, dc, :n], in0=ps[:, bk, :n], in1=en[:, dc, :n],
                                                              op=ALU.mult),
                             reads=[psb[bk], enb[dc]], writes=[ktb[dc]])
                        for ci, (o, nt) in enumerate(chunks):
                            if cpre[ci]:
                                S.op("dve", lambda e: e.tensor_tensor(out=kt[:, dc, o:o + nt], in0=kt[:, dc, o:o + nt],
                                                                      in1=cst_t[:, C_GM:C_GM + nt], op=ALU.mult),
                                     reads=[ktb[dc], b_cst], writes=[ktb[dc]])
                release()
                for half in range(2):
                    sv = acquire(("cols", "gla_w_v", hd * 512 + half * 256))
                    for ci, (o, nt) in enumerate(chunks):
                        bv = bank(GP)
                        S.mm([lambda e, kc=kc: e.matmul(ps[:nt, bv, :256], lhsT=gh[:, kc, o:o + nt],
                                                        rhs=ring_t[:, sv, kc * 256:(kc + 1) * 256],
                                                        start=(kc == 0), stop=(kc == KC - 1)) for kc in range(KC)],
                             reads=ghb, writes=[psb[bv]])
                        S.op("act", lambda e: e.activation(out=vt[:nt, ci, half * 256:(half + 1) * 256],
                                                           in_=ps[:nt, bv, :256], func=AF.Copy),
                             reads=[psb[bv]], writes=[vtb[ci]])
                    release()
                if phaseB:
                    sq_ = acquire(("cols", "gla_w_q", hd * 256))
                    for dc in range(2):
                        bq_ = bank(GP)
                        S.mm([lambda e, kc=kc: e.matmul(ps[:, bq_, :n], lhsT=ring_t[:, sq_, kc * 256 + dc * 128:kc * 256 + dc * 128 + 128],
                                                        rhs=gh[:, kc, :n], start=(kc == 0), stop=(kc == KC - 1))
                              for kc in range(KC)],
                             reads=ghb, writes=[psb[bq_]])
                        S.op("dve", lambda e: e.scalar_tensor_tensor(out=qt[:, dc, :n], in0=ps[:, bq_, :n], scalar=0.0625,
                                                                     in1=ep[:, dc, :n], op0=ALU.mult, op1=ALU.mult),
                             reads=[psb[bq_], epb[dc]], writes=[qtb[dc]])
                    release()
                    for half in range(2):
                        sr = acquire(("cols", "gla_w_r", hd * 512 + half * 256))
                        for vv in range(2):
                            vc = half * 2 + vv
                            br = bank(GP)
                            S.mm([lambda e, kc=kc: e.matmul(ps[:, br, :n], lhsT=ring_t[:, sr, kc * 256 + vv * 128:kc * 256 + vv * 128 + 128],
                                                            rhs=gh[:, kc, :n], start=(kc == 0), stop=(kc == KC - 1))
                                  for kc in range(KC)],
                                 reads=ghb, writes=[psb[br]])
                            S.op("act", lambda e: e.activation(out=rr[:, vc, :n], in_=ps[:, br, :n], func=AF.Silu),
                                 reads=[psb[br]], writes=[rrb[vc]])
                        release()
                    for ci, (o, nt) in enumerate(chunks):
                        bs_ = bank(GP)
                        S.mm([lambda e, dc=dc: e.matmul(ps[:nt, bs_, :nt], lhsT=kt[:, dc, o:o + nt], rhs=qt[:, dc, o:o + nt],
                                                        start=(dc == 0), stop=(dc == 1)) for dc in range(2)],
                             reads=ktb + qtb, writes=[psb[bs_]])
                        S.op("dve", lambda e: e.tensor_tensor(out=sc[:nt, ci, :nt], in0=ps[:nt, bs_, :nt], in1=Um[:nt, :nt],
                                                              op=ALU.mult),
                             reads=[psb[bs_], b_cst], writes=[scb[ci]])
                if samp:
                    BO = 5
                    S.mm([lambda e, vc=vc: e.matmul(ps[:, BO, vc * 128:vc * 128 + 64], lhsT=vt[:64, 0, vc * 128:(vc + 1) * 128],
                                                    rhs=sc[:64, 0, :64], start=(vc == 0), stop=False, skip_group_check=True)
                          for vc in range(4)],
                         reads=[vtb[0], scb[0]], writes=[psb[BO]])
                    def ld_state(q_):
                        S.dma("sp", s0f[q_ % 4][:], sgla[q_, hd, :, :].rearrange("(c p) v -> p c v", p=128), writes=[s0fb[q_ % 4]])
                    for q_ in range(3):
                        ld_state(q_)
                    for s_ in range(16):
                        bi2 = s_ % 4
                        sbi = s_ % 2
                        if s_ + 3 < 16:
                            ld_state(s_ + 3)
                        S.op("act", lambda e: e.activation(out=Sbs[sbi][:], in_=s0f[bi2][:], func=AF.Copy),
                             reads=[s0fb[bi2]], writes=[Sbsb[sbi]])
                        fns = []
                        for vc in range(4):
                            for dc in range(2):
                                last = (s_ == 15 and vc == 3 and dc == 1)
                                fns.append(lambda e, vc=vc, dc=dc, last=last: e.matmul(
                                    ps[:, BO, vc * 128 + s_ * 4:vc * 128 + s_ * 4 + 4], lhsT=Sbs[sbi][:, dc, vc * 128:(vc + 1) * 128],
                                    rhs=qt[:, dc, s_ * 4:s_ * 4 + 4], start=False, stop=last, skip_group_check=True))
                        S.mm(fns, reads=[Sbsb[sbi]] + qtb, writes=[psb[BO]])
                        S.op("dve", lambda e: e.tensor_scalar(out=kh[:64, 1, :], in0=kh[:64, 0, :],
                                                              scalar1=cst_t[:64, C_SEQ + s_:C_SEQ + s_ + 1], scalar2=None, op0=ALU.mult),
                             reads=[khb[0], b_cst], writes=[khb[1]])
                        for dc in range(2):
                            bu_ = bank(GP)
                            S.mm([lambda e: e.matmul(ps[:, bu_, :512], lhsT=kh[:64, 1, dc * 128:(dc + 1) * 128], rhs=vt[:64, 0, :],
                                                     start=True, stop=True)],
                                 reads=[khb[1], vtb[0]], writes=[psb[bu_]])
                            S.op("dve", lambda e: e.scalar_tensor_tensor(
                                out=snw[bi2][:, dc, :], in0=s0f[bi2][:, dc, :], scalar=ep[:, dc, s_ * 4 + 3:s_ * 4 + 4],
                                in1=ps[:, bu_, :512], op0=ALU.mult, op1=ALU.add),
                                reads=[s0fb[bi2], epb[dc], psb[bu_]], writes=[snwb[bi2]])
                        S.dma("sp", ngs_out[s_, hd, :, :].rearrange("(c p) v -> p c v", p=128), snw[bi2][:],
                              reads=[snwb[bi2]], is_output=True)
                    S.op("act", lambda e: e.activation(out=oT[:, :, :64],
                                                       in_=ps[:, BO, :].rearrange("p (v t) -> p v t", v=4)[:, :, :64], func=AF.Copy),
                         reads=[psb[BO]], writes=[oTb])
                else:
                    for ci, (o, nt) in enumerate(chunks):
                        if phaseB:
                            bo = bank(GP)
                            fns = []
                            for vc in range(4):
                                for dc in range(2):
                                    fns.append(lambda e, vc=vc, dc=dc: e.matmul(
                                        ps[:, bo, vc * 128:vc * 128 + nt], lhsT=Sb[:, dc, vc * 128:(vc + 1) * 128],
                                        rhs=qt[:, dc, o:o + nt], start=(vc == 0 and dc == 0), stop=False, skip_group_check=True))
                                fns.append(lambda e, vc=vc: e.matmul(
                                    ps[:, bo, vc * 128:vc * 128 + nt], lhsT=vt[:nt, ci, vc * 128:(vc + 1) * 128],
                                    rhs=sc[:nt, ci, :nt], start=False, stop=(vc == 3), skip_group_check=True))
                            S.mm(fns, reads=Sbb + qtb + [vtb[ci], scb[ci]], writes=[psb[bo]])
                            S.op("act", lambda e: e.activation(
                                out=oT[:, :, o:o + nt], in_=ps[:, bo, :].rearrange("p (v t) -> p v t", v=4)[:, :, :nt], func=AF.Copy),
                                reads=[psb[bo]], writes=[oTb])
                        for dc in range(2):
                            bu_ = bank(GP)
                            S.mm([lambda e: e.matmul(ps[:, bu_, :512], lhsT=kh[:nt, ci, dc * 128:(dc + 1) * 128], rhs=vt[:nt, ci, :],
                                                     start=True, stop=True)],
                                 reads=[khb[ci], vtb[ci]], writes=[psb[bu_]])
                            S.op("dve", lambda e: e.scalar_tensor_tensor(
                                out=Sst[:, hd * 2 + dc, :], in0=Sst[:, hd * 2 + dc, :], scalar=ep[:, dc, o + nt - 1:o + nt],
                                in1=ps[:, bu_, :512], op0=ALU.mult, op1=ALU.add),
                                reads=[Sstb[hd * 2 + dc], epb[dc], psb[bu_]], writes=[Sstb[hd * 2 + dc]])
                            if phaseB and ci + 1 < len(chunks):
                                S.op("act", lambda e: e.activation(out=Sb[:, dc, :], in_=Sst[:, hd * 2 + dc, :], func=AF.Copy),
                                     reads=[Sstb[hd * 2 + dc]], writes=[Sbb[dc]])
                if not phaseB:
                    continue
                for vc in range(4):
                    S.op("act", lambda e: e.activation(out=osq[vc][:, :n], in_=oT[:, vc, :n], func=AF.Square),
                         reads=[oTb], writes=[osqb[vc]])
                S.mm([lambda e, vc=vc: e.matmul(ps[:, 6, :n], lhsT=ones_b[:], rhs=osq[vc][:, :n], start=(vc == 0), stop=(vc == 3))
                      for vc in range(4)],
                     reads=osqb + [b_ones], writes=[psb[6]])
                S.op("act", lambda e: e.activation(out=grt[:, :n], in_=ps[:, 6, :n], func=AF.Sqrt, scale=1.0 / 512, bias=EPS),
                     reads=[psb[6]], writes=[b_grt])
                S.op("dve", lambda e: e.reciprocal(out=grt[:, :n], in_=grt[:, :n]), reads=[b_grt], writes=[b_grt])
                for vc in range(4):
                    ti = vc % 2
                    S.op("dve", lambda e: e.scalar_tensor_tensor(out=t1[ti][:, :n], in0=oT[:, vc, :n], scalar=vcol(V_GN + vc),
                                                                 in1=grt[:, :n], op0=ALU.mult, op1=ALU.mult),
                         reads=[oTb, b_grt, b_vec], writes=[t1b[ti]])
                    S.op("dve", lambda e: e.tensor_tensor(out=og[:, hd * 4 + vc, :n], in0=t1[ti][:, :n], in1=rr[:, vc, :n],
                                                          op=ALU.mult),
                         reads=[t1b[ti], rrb[vc]], writes=[ogb[hd * 4 + vc]])
            if not phaseB or part == "heads":
                return
            for o2 in range(8):
                so = acquire(("cols", "gla_w_o", o2 * 256))
                for oo in range(2):
                    oc = o2 * 2 + oo
                    bo = bank(GP)
                    S.mm([lambda e, kc=kc: e.matmul(ps[:, bo, :n], lhsT=ring_t[:, so, kc * 256 + oo * 128:kc * 256 + oo * 128 + 128],
                                                    rhs=og[:, kc, :n], start=(kc == 0), stop=(kc == KC - 1)) for kc in range(KC)],
                         reads=ogb, writes=[psb[bo]])
                    xw = [xb[oc][s] for s in segs]
                    S.op("dve", lambda e: e.tensor_tensor(out=x[:, oc, c0:c0 + n], in0=ps[:, bo, :n], in1=x[:, oc, c0:c0 + n],
                                                          op=ALU.add),
                         reads=[psb[bo]] + xw, writes=xw)
                release()

        if mode == "A":
            for (c0, n, ch3) in GLA_BLOCKS:
                gla_block(c0, n, segs_of(c0, n), False, ch3, False)
            ev = S.dma("pool", cc_src.ap().rearrange("(g p) v -> p g v", p=128), Sst[:], reads=Sstb)
            S.wait_event("pool", ev)
            nc.gpsimd.collective_compute(
                "AllGather", ALU.bypass, replica_groups=[[0, 1], [2, 3], [4, 5], [6, 7]],
                ins=[cc_src.ap().opt()], outs=[cc_dst.ap().opt()]).then_inc(S.semh["cc"])
            S.cnt["cc"] += 1
        elif mode == "S":
            gla_block(CS, TS, segs_of(CS, TS), True, ((0, TS, False),), True)
        else:
            S.wait_event("pool", ("cc", S.cnt["cc"]))
            S.dma("pool", Sst[:], cc_dst.ap()[0:1024, :].rearrange("(g p) v -> p g v", p=128), reads=Sstb, writes=Sstb)
            S.op("dve", lambda e: e.tensor_scalar(out=Sst[:], in0=Sst[:], scalar1=vcol(V_FLAG), scalar2=None, op0=ALU.mult),
                 reads=Sstb + [b_vec], writes=Sstb)
            for bi_, (c0, n, ch3) in enumerate(GLA_BLOCKS):
                if bi_ == 0:
                    gla_block(c0, n, segs_of(c0, n), False, ch3, True, "pro")
                gla_block(c0, n, segs_of(c0, n), False, ch3, True, "heads")
                if bi_ + 1 < len(GLA_BLOCKS):
                    c1, n1, ch1 = GLA_BLOCKS[bi_ + 1]
                    gla_block(c1, n1, segs_of(c1, n1), False, ch1, True, "pro")
                gla_block(c0, n, segs_of(c0, n), False, ch3, True, "wo")
            S.dma("sp", ngp_out.rearrange("(g p) v -> p g v", p=128), Sst[:], reads=Sstb, is_output=True)
        S.barrier()
        free(len(ctx) - n0)

    def final_out():
        gcol = 96
        sqt = [alloc("fsq%d" % a, [128, T], BF16) for a in range(2)]
        sqb = bufs(2)
        rt = alloc("frt", [128, T], F32)
        rtb = Buf()
        yt = alloc("yt", [128, KC, 128], F32)
        ytb = bufs(KC)
        tko = [alloc("tko%d" % a, [128, D], F32) for a in range(2)]
        tkb = bufs(2)
        rms_stats_full(sqt, sqb, rt, rtb)
        tpool = (0, 1, 2, 3, 4)
        for ti, (r0, nt) in enumerate(TT):
            for kc in range(KC):
                S.op("dve", lambda e, kc=kc: e.scalar_tensor_tensor(
                    out=yt[:, kc, :nt], in0=x[:, kc, r0:r0 + nt], scalar=vcol(gcol + kc),
                    in1=rt[:, r0:r0 + nt], op0=ALU.mult, op1=ALU.mult),
                    reads=xb[kc] + [rtb, b_vec], writes=[ytb[kc]])
            tk, tb = tko[ti % 2], tkb[ti % 2]
            for k4 in range(4):
                bi = bank(tpool)
                fns = [lambda e, j=j: e.transpose(ps[:nt, bi, j * 128:(j + 1) * 128], yt[:, k4 * 4 + j, :nt], ident[:, :])
                       for j in range(4)]
                S.mm(fns, reads=[ytb[k4 * 4 + j] for j in range(4)] + [b_cst], writes=[psb[bi]])
                S.op("act", lambda e: e.activation(out=tk[:nt, k4 * 512:(k4 + 1) * 512], in_=ps[:nt, bi, :], func=AF.Copy),
                     reads=[psb[bi]], writes=[tb])
            S.dma("sp", y_out[r0:r0 + nt, :], tk[:nt, :], reads=[tb], is_output=True)
        for ev in S.out_events:
            S.wait_event("sp", ev)
        free(6)

    stages = STAGES.split(",")
    full = "all" in stages
    if full or "ffn" in stages:
        ffn(0, 0)
    if full or "conv" in stages:
        conv_mixer()
        S.dma("act", ncs_out[:, 0:26, :], sconv[:, 4:30, :], is_output=True)
    if full:
        ffn(0, 1)
        ffn(1, 0)
    if full or "gla" in stages:
        gla_mixer()
    if full:
        ffn(1, 1, [(TP, 512 - TP, segs_of(TP, 512 - TP))] + BLKS[1:])
    final_out()
    pcm.__exit__(None, None, None)
    while ctx:
        free(1)
    S.close()
    if record:
        return rec_keys
    assert ring.n_acq == len(plan_keys), (ring.n_acq, len(plan_keys))
    return nc


_CACHE = {}


def _get_prog():
    if "prog" not in _CACHE:
        keys = build_program(None)
        nc = build_program(keys)
        uniq = {}
        for k in keys:
            if k[0] != "state" and k not in uniq:
                uniq[k] = len(uniq)
        _CACHE["prog"] = (nc, uniq)
    return _CACHE["prog"]


def _vec16(v):
    return np.asarray(v, np.float32).reshape(16, 128).T


def kernel(**inputs):
    inp = {k: np.asarray(v) for k, v in inputs.items()}
    nc, uniq = _get_prog()
    wpack = np.zeros((max(1, len(uniq)), 128, UE), np.float32)
    for key, idx in uniq.items():
        wpack[idx] = materialize(key, inp)
    vecs0 = np.zeros((128, 256), np.float32)
    for i in range(2):
        for j in range(2):
            vecs0[:, (i * 2 + j) * 16:(i * 2 + j) * 16 + 16] = _vec16(inp["norm_ffn"][i, j])
    vecs0[:, 64:80] = _vec16(inp["norm_mix"][0])
    vecs0[:, 80:96] = _vec16(inp["norm_mix"][1])
    vecs0[:, 96:112] = _vec16(inp["norm_final"])
    vecs0[:, V_B1V:V_B1V + 16] = _vec16(inp["conv_b_pw1"][0, :2048])
    vecs0[:, V_B1G:V_B1G + 16] = _vec16(inp["conv_b_pw1"][0, 2048:])
    vecs0[:, V_BDW:V_BDW + 16] = _vec16(inp["conv_b_dw"][0])
    vecs0[:, V_LNG:V_LNG + 16] = _vec16(inp["conv_ln_g"][0])
    vecs0[:, V_LNB:V_LNB + 16] = _vec16(inp["conv_ln_b"][0])
    vecs0[:, V_B2:V_B2 + 16] = _vec16(inp["conv_b_pw2"][0])
    vecs0[:, V_GN:V_GN + 4] = np.asarray(inp["gla_gn_g"][0], np.float32).reshape(4, 128).T
    cst0 = np.zeros((128, 1024), np.float32)
    ii = np.arange(128)
    cst0[:, C_ID:C_ID + 128] = np.eye(128, dtype=np.float32)
    cst0[:, C_U:C_U + 128] = (ii[:, None] <= ii[None, :])
    cst0[:, C_M:C_M + 128] = (ii[:, None] > ii[None, :])
    i64 = np.arange(64)
    same = (i64[:, None] // 4) == (i64[None, :] // 4)
    cst0[:64, C_US:C_US + 64] = same & (i64[:, None] <= i64[None, :])
    cst0[:64, C_MS:C_MS + 64] = same & (i64[:, None] > i64[None, :])
    cst0[:64, C_SEQ:C_SEQ + 16] = (i64[:, None] // 4) == np.arange(16)[None, :]
    wdw = np.ascontiguousarray(inp["conv_w_dw"][0].reshape(31, 16, 128).transpose(2, 1, 0)).reshape(128, 496)
    wg1 = np.ascontiguousarray(inp["gla_w_g1"][0].reshape(16, 128, 16).transpose(1, 0, 2)).reshape(128, 256)
    wg2a = np.zeros((32, 1024), np.float32)
    wg2a[0:16] = inp["gla_w_g2"][0]
    wg2a[16] = inp["gla_b_g"][0]
    xp, xs, meta = inp["x_prompt"], inp["x_sample"], inp["meta_tokens"]
    in_maps = []
    for c in range(8):
        b, hf = c // 2, c % 2
        xc = np.zeros((T, D), np.float32)
        xc[CM:CM + TM] = xp[b, hf * TM:(hf + 1) * TM]
        vecs = vecs0.copy()
        cst = cst0.copy()
        if hf == 0:
            xc[CP + TP - 16:CP + TP] = meta
            cst[:, C_CM + TP - 16:C_CM + TP] = 1.0
            cst[:, C_GM + TP - 16:C_GM + TP] = 1.0
            vecs[TP - 16:TP, V_GMT] = 1.0
        else:
            xc[CP:CP + TP] = xp[b, TM - TP:TM]
            cst[:, C_CM:C_CM + TP] = 1.0
            vecs[:, V_FLAG] = 1.0
        xc[CS:CS + TS] = xs[16 * c:16 * c + 16].reshape(TS, D)
        in_maps.append({
            "x_in": xc, "vecs": vecs, "cst": cst, "wdw": wdw, "wg1": wg1, "wg2a": wg2a, "wpack": wpack,
            "sconv": np.ascontiguousarray(inp["state_conv"][0, 16 * c:16 * c + 16]),
            "sgla": np.ascontiguousarray(inp["state_gla"][0, 16 * c:16 * c + 16]),
        })
    res = run_bass_kernel_spmd(nc, in_maps, core_ids=list(range(8)))
    yp = np.zeros((4, 2048, D), np.float32)
    ys = np.zeros((128, 4, D), np.float32)
    ncp = np.zeros((1, 4, 30, D), np.float32)
    ngp = np.zeros((1, 4, 4, 256, 512), np.float32)
    ncs = np.zeros((1, 128, 30, D), np.float32)
    ngs = np.zeros((1, 128, 4, 256, 512), np.float32)
    for c in range(8):
        b, hf = c // 2, c % 2
        r = res.results[c]
        yo = r["y_out"]
        yp[b, hf * TM:(hf + 1) * TM] = yo[CM:CM + TM]
        ys[16 * c:16 * c + 16] = yo[CS:CS + TS].reshape(16, 4, D)
        if hf == 1:
            ncp[0, b] = r["ncp_out"]
            ngp[0, b] = r["ngp_out"].reshape(4, 256, 512)
        ncs[0, 16 * c:16 * c + 16] = r["ncs_out"]
        ngs[0, 16 * c:16 * c + 16] = r["ngs_out"]
    return yp, ys, ncp, ngp, ncs, ngs
```

```python
import os
import numpy as np
import concourse.bass as bass
import concourse.mybir as mybir
from concourse.bass_utils import run_bass_kernel_spmd

F32 = mybir.dt.float32
BF16 = mybir.dt.bfloat16
AF = mybir.ActivationFunctionType
ALU = mybir.AluOpType

D = 2048
KC = 16
DFF = 5632
NG = 11
TM, TP, TS = 1024, 32, 64
T = TM + TP + TS
CP, CM, CS = 0, TP, TP + TM
CONV_BLOCKS = [(0, 352, TP), (352, 352, 0), (704, 352, 0)]
GLA_BLOCKS = [(0, 288, ((0, 32, True), (32, 128, False), (160, 128, False)))] + \
             [(288 + i * 256, 256, ((0, 128, False), (128, 128, False))) for i in range(3)]
GLA_BLOCKS_A = [(0, 544, ((0, 32, True),) + tuple((32 + i * 128, 128, False) for i in range(4))),
                (544, 512, tuple((i * 128, 128, False) for i in range(4)))]
_BND = sorted({0, T} | {b[0] for b in CONV_BLOCKS} | {b[0] for b in GLA_BLOCKS} | {512, 1024, CS})
SEGS = [(_BND[i], _BND[i + 1] - _BND[i]) for i in range(len(_BND) - 1)]


def segs_of(c0, n):
    return tuple(i for i, (a, w) in enumerate(SEGS) if a < c0 + n and a + w > c0)


BLKS = [(0, 512, segs_of(0, 512)), (512, 512, segs_of(512, 512)), (1024, T - 1024, segs_of(1024, T - 1024))]
NSLOT = 6
UE = 4096
EPS = 1e-6
NQ = 8
STAGES = os.environ.get("MK_STAGES", "all")


class Buf:
    __slots__ = ("w", "r")

    def __init__(self):
        self.w = None
        self.r = {}


def bufs(n):
    return [Buf() for _ in range(n)]


class Sched:
    def __init__(self, nc):
        self.nc = nc
        self.eng = {"pe": nc.tensor, "act": nc.scalar, "dve": nc.vector, "pool": nc.gpsimd, "sp": nc.sync}
        self.semh = {}
        self.cnt = {}
        self.seen = {e: {} for e in self.eng}
        self._cms = []
        for e in self.eng:
            self._newsem(e)
        self.dq = {}
        for q in ("sp", "pool", "act"):
            names = []
            for i in range(NQ):
                nm = "d_%s_%d" % (q, i)
                self._newsem(nm)
                names.append(nm)
            self.dq[q] = [names, 0]
        self.out_events = []

    def _newsem(self, name):
        cm = self.nc.semaphore("s_" + name)
        h = cm.__enter__()
        self._cms.append(cm)
        self.semh[name] = h
        self.cnt[name] = 0

    def close(self):
        for cm in reversed(self._cms):
            cm.__exit__(None, None, None)

    def _deps(self, reads, writes):
        deps = {}
        for b in reads:
            if b.w is not None:
                s, v = b.w
                if deps.get(s, 0) < v:
                    deps[s] = v
        for b in writes:
            if b.w is not None:
                s, v = b.w
                if deps.get(s, 0) < v:
                    deps[s] = v
            for s, v in b.r.items():
                if deps.get(s, 0) < v:
                    deps[s] = v
        return deps

    def _wait(self, e, deps):
        eng = self.eng[e]
        seen = self.seen[e]
        for s, v in deps.items():
            if seen.get(s, 0) < v:
                eng.wait_ge(self.semh[s], v)
                seen[s] = v

    def wait_event(self, e, ev):
        self._wait(e, {ev[0]: ev[1]})

    def _mark(self, ev, reads, writes):
        s, v = ev
        for b in writes:
            b.w = ev
            b.r = {}
        for b in reads:
            if b.r.get(s, 0) < v:
                b.r[s] = v

    def op(self, e, fn, reads=(), writes=()):
        self._wait(e, self._deps(reads, writes))
        ins = fn(self.eng[e])
        self.cnt[e] += 1
        ins.then_inc(self.semh[e], 1)
        ev = (e, self.cnt[e])
        self._mark(ev, reads, writes)
        return ev

    def mm(self, fns, reads=(), writes=()):
        deps = self._deps(reads, writes)
        deps.pop("pe", None)
        self._wait("pe", deps)
        ins = None
        for fn in fns:
            ins = fn(self.nc.tensor)
        self.cnt["pe"] += 1
        ins.then_inc(self.semh["pe"], 1)
        ev = ("pe", self.cnt["pe"])
        self._mark(ev, reads, writes)
        return ev

    def dma(self, q, out, in_, reads=(), writes=(), is_output=False):
        names, idx = self.dq[q]
        nm = names[idx % NQ]
        self.dq[q][1] = idx + 1
        deps = self._deps(reads, writes)
        if self.cnt[nm] > 0:
            deps[nm] = max(deps.get(nm, 0), self.cnt[nm])
        self._wait(q, deps)
        self.cnt[nm] += 16
        self.eng[q].dma_start(out=out, in_=in_).then_inc(self.semh[nm], 16)
        ev = (nm, self.cnt[nm])
        self._mark(ev, reads, writes)
        if is_output:
            self.out_events.append(ev)
        return ev

    def barrier(self):
        evs = {e: self.cnt[e] for e in ("pe", "act", "dve") if self.cnt[e] > 0}
        for q in self.dq:
            for nm in self.dq[q][0]:
                if self.cnt[nm] > 0:
                    evs[nm] = self.cnt[nm]
        for e in ("pe", "act", "dve", "sp"):
            self._wait(e, dict(evs))


class Ring:
    def __init__(self, S, ring_tile):
        self.S = S
        self.tile = ring_tile
        self.specs = []
        self.n_issued = 0
        self.n_acq = 0
        self.sems = []
        for i in range(NSLOT):
            nm = "ring%d" % i
            S._newsem(nm)
            self.sems.append(nm)
        self.plan = None
        self.rel_ev = {}
        self.fill_val = {}
        self.conv_jobs = []
        self.conv_total = 0
        S._newsem("wconv")
        self.live = []

    def set_plan(self, plan, scratch_flags, conv_jobs):
        self.plan = plan
        self.plan_scratch = scratch_flags
        self.conv_jobs = list(conv_jobs)

    def _issue(self, u):
        S = self.S
        slot = u % NSLOT
        if u >= NSLOT:
            S.wait_event("pool", self.rel_ev[u - NSLOT])
        nm = self.sems[slot]
        if self.plan_scratch[u]:
            assert not self.conv_jobs, "scratch unit issued before all conversions were queued"
            S.wait_event("pool", ("wconv", S.cnt["wconv"]))
        for (src_ap, dst_fn) in self.plan[u]:
            S.cnt[nm] += 16
            S.nc.gpsimd.dma_start(out=dst_fn(self.tile, slot), in_=src_ap).then_inc(S.semh[nm], 16)
        self.fill_val[u] = S.cnt[nm]
        if u >= NSLOT and self.conv_jobs:
            dst, src = self.conv_jobs.pop(0)
            S.cnt["wconv"] += 16
            S.nc.gpsimd.dma_start(out=dst, in_=src).then_inc(S.semh["wconv"], 16)

    def prime(self):
        while self.n_issued < min(NSLOT, len(self.plan)):
            self._issue(self.n_issued)
            self.n_issued += 1

    def acquire(self):
        u = self.n_acq
        assert u < len(self.plan), "ring plan exhausted"
        self.n_acq += 1
        slot = u % NSLOT
        assert u in self.fill_val, "unit acquired before its DMA was issued"
        self.S.wait_event("pe", (self.sems[slot], self.fill_val[u]))
        self.live.append(u)
        return self.tile, slot

    def release_all(self):
        ev = ("pe", self.S.cnt["pe"])
        for u in self.live:
            self.rel_ev[u] = ev
        self.live = []
        while self.n_issued < len(self.plan) and (self.n_issued - NSLOT) in self.rel_ev:
            self._issue(self.n_issued)
            self.n_issued += 1


class Plan:
    def __init__(self):
        self.keys = []
        self.uniq = {}

    def add(self, key):
        self.keys.append(key)
        if key[0] != "state" and key not in self.uniq:
            self.uniq[key] = len(self.uniq)


def ffn_plan(P, i, j):
    for g in range(NG + 1):
        for hf in range(2):
            if g < NG:
                P.add(("gate", i, j, g, hf))
                P.add(("up", i, j, g, hf))
            if g > 0:
                P.add(("down", i, j, g - 1, hf))


def materialize(key, inp):
    k = key[0]
    if k in ("gate", "up"):
        _, i, j, g, hf = key
        W = inp["w_ffn_gate" if k == "gate" else "w_ffn_up"][i, j]
        c0 = g * 512 + hf * 256
        return W[:, c0:c0 + 256].reshape(16, 128, 256).transpose(1, 0, 2).reshape(128, UE)
    if k == "down":
        _, i, j, g, hf = key
        W = inp["w_ffn_down"][i, j]
        return W[g * 512:(g + 1) * 512, hf * 1024:(hf + 1) * 1024].reshape(4, 128, 1024).transpose(1, 0, 2).reshape(128, UE)
    if k == "cols":
        _, name, c0 = key
        W = inp[name][0]
        return W[:, c0:c0 + 256].reshape(16, 128, 256).transpose(1, 0, 2).reshape(128, UE)
    if k == "pw1":
        _, c = key
        W = inp["conv_w_pw1"][0]
        a = np.stack([W[:, c * 128:(c + 1) * 128], W[:, 2048 + c * 128:2048 + (c + 1) * 128]], axis=1)
        return a.reshape(16, 128, 2, 128).transpose(1, 0, 2, 3).reshape(128, UE)
    raise KeyError(key)


NBC = 352
NBG = 288
V_B1V, V_B1G, V_BDW, V_LNG, V_LNB, V_B2, V_GN, V_FLAG, V_GMT = 112, 128, 144, 160, 176, 192, 208, 212, 213
C_ID, C_U, C_M, C_US, C_MS, C_SEQ, C_CM, C_GM = 0, 128, 256, 384, 448, 512, 528, 600


def build_program(plan_keys=None):
    record = plan_keys is None
    nc = bass.Bass("TRN2", target_bir_lowering=False)
    rec_keys = []
    uniq = {}
    if not record:
        for k in plan_keys:
            if k[0] != "state" and k not in uniq:
                uniq[k] = len(uniq)
    NU = max(1, len(uniq))
    x_in = nc.dram_tensor("x_in", [T, D], F32, kind="ExternalInput").ap()
    vecs = nc.dram_tensor("vecs", [128, 256], F32, kind="ExternalInput").ap()
    cst = nc.dram_tensor("cst", [128, 1024], F32, kind="ExternalInput").ap()
    wdw_d = nc.dram_tensor("wdw", [128, 16 * 31], F32, kind="ExternalInput").ap()
    wg1_d = nc.dram_tensor("wg1", [128, 16 * 16], F32, kind="ExternalInput").ap()
    wg2_d = nc.dram_tensor("wg2a", [32, 1024], F32, kind="ExternalInput").ap()
    sconv = nc.dram_tensor("sconv", [16, 30, D], F32, kind="ExternalInput").ap()
    sgla = nc.dram_tensor("sgla", [16, 4, 256, 512], F32, kind="ExternalInput").ap()
    wpack = nc.dram_tensor("wpack", [NU, 128, UE], F32, kind="ExternalInput").ap()
    y_out = nc.dram_tensor("y_out", [T, D], F32, kind="ExternalOutput").ap()
    ncp_out = nc.dram_tensor("ncp_out", [30, D], F32, kind="ExternalOutput").ap()
    ngp_out = nc.dram_tensor("ngp_out", [1024, 512], F32, kind="ExternalOutput").ap()
    ncs_out = nc.dram_tensor("ncs_out", [16, 30, D], F32, kind="ExternalOutput").ap()
    ngs_out = nc.dram_tensor("ngs_out", [16, 4, 256, 512], F32, kind="ExternalOutput").ap()
    mix_keys = []
    if not record:
        for k in plan_keys:
            if k[0] in ("pw1", "cols") and k not in mix_keys:
                mix_keys.append(k)
    mix_idx = {k: i for i, k in enumerate(mix_keys)}
    wbf = nc.dram_tensor("wbf", [max(1, len(mix_keys)), 128, UE], BF16)
    cc_src = nc.dram_tensor("cc_src", [1024, 512], F32)
    cc_dst = nc.dram_tensor("cc_dst", [2048, 512], F32)

    S = Sched(nc)
    S._newsem("cc")
    ctx = []

    alloc_id = [0]

    def alloc(name, shape, dt):
        alloc_id[0] += 1
        cm = nc.sbuf_tensor("%s_%d" % (name, alloc_id[0]), shape, dt)
        t = cm.__enter__()
        ctx.append(cm)
        return t

    def free(n):
        for _ in range(n):
            ctx.pop().__exit__(None, None, None)

    x = alloc("x", [128, KC, T], F32)
    xb = [bufs(len(SEGS)) for _ in range(KC)]
    ring_t = alloc("ring", [128, NSLOT, UE], BF16)
    vec_t = alloc("vec_t", [128, 256], F32)
    cst_t = alloc("cst_t", [128, 1024], F32)
    ones_b = alloc("ones_b", [128, 128], BF16)
    b_vec, b_cst, b_ones = Buf(), Buf(), Buf()
    pcm = nc.psum_tensor("ps", [128, 8, 512], F32)
    ps = pcm.__enter__()
    psb = bufs(8)
    bank_rr = [0]
    ALLB = (0, 1, 2, 3, 4, 5, 6, 7)
    POOL6 = (0, 1, 2, 3, 4, 5)
    POOL5 = (0, 1, 2, 3, 4)

    def bank(pool=ALLB):
        i = pool[bank_rr[0] % len(pool)]
        bank_rr[0] += 1
        return i

    ring = Ring(S, ring_t)
    if not record:
        plan_aps = []
        scratch_flags = []
        for key in plan_keys:
            scratch_flags.append(key in mix_idx)
            if key in mix_idx:
                plan_aps.append([(wbf.ap()[mix_idx[key], :, :], lambda tile, slot: tile[:, slot, :])])
            elif key[0] == "state":
                _, gq, hd = key
                lst = []
                for s4 in range(4):
                    src = sgla[gq * 4 + s4, hd, :, :].rearrange("(c p) v -> p c v", p=128)
                    lst.append((src, lambda tile, slot, s4=s4: tile[:, slot, s4 * 1024:(s4 + 1) * 1024].rearrange(
                        "p (c v) -> p c v", c=2)))
                plan_aps.append(lst)
            else:
                plan_aps.append([(wpack[uniq[key], :, :], lambda tile, slot: tile[:, slot, :])])
        conv_jobs = [(wbf.ap()[mix_idx[k], :, :], wpack[uniq[k], :, :]) for k in mix_keys]
        ring.set_plan(plan_aps, scratch_flags, conv_jobs)
        ring.prime()
    rec_state = {"n": 0}

    def acquire(key):
        if record:
            rec_keys.append(key)
            u = rec_state["n"]
            rec_state["n"] += 1
            return u % NSLOT
        assert plan_keys[ring.n_acq] == key, (ring.n_acq, plan_keys[ring.n_acq], key)
        _, slot = ring.acquire()
        return slot

    def release():
        if not record:
            ring.release_all()

    ident = cst_t[:, C_ID:C_ID + 128]
    S.dma("sp", vec_t[:], vecs[:, :], writes=[b_vec])
    S.dma("sp", cst_t[:], cst[:, :], writes=[b_cst])
    S.op("dve", lambda e: e.memset(ones_b[:], 1.0), writes=[b_ones])

    def vcol(c):
        return vec_t[:, c:c + 1]

    TT = [(i * 128, 128) for i in range(8)] + [(1024, TP + TS)]
    tokin = [alloc("tokin%d" % i, [128, D], F32) for i in range(4)]
    tokb = bufs(4)
    for ti, (r0, nt) in enumerate(TT):
        tk, tb = tokin[ti % 4], tokb[ti % 4]
        S.dma("sp", tk[:nt, :], x_in[r0:r0 + nt, :], writes=[tb])
        segs = segs_of(r0, nt)
        for k4 in range(4):
            bi = bank()
            fns = []
            for j in range(4):
                kc = k4 * 4 + j
                fns.append(lambda e, kc=kc, j=j, bi=bi, tk=tk, nt=nt: e.transpose(
                    ps[:, bi, j * 128:j * 128 + nt], tk[:nt, kc * 128:(kc + 1) * 128], ident[:nt, :nt]))
            S.mm(fns, reads=[tb, b_cst], writes=[psb[bi]])
            wr = [xb[k4 * 4 + j][s] for j in range(4) for s in segs]
            if k4 % 2:
                S.op("act", lambda e, bi=bi, k4=k4, r0=r0, nt=nt: e.activation(
                    out=x[:, k4 * 4:k4 * 4 + 4, r0:r0 + nt],
                    in_=ps[:, bi, :].rearrange("p (j t) -> p j t", j=4)[:, :, :nt], func=AF.Copy),
                    reads=[psb[bi]], writes=wr)
            else:
                S.op("dve", lambda e, bi=bi, k4=k4, r0=r0, nt=nt: e.tensor_copy(
                    out=x[:, k4 * 4:k4 * 4 + 4, r0:r0 + nt],
                    in_=ps[:, bi, :].rearrange("p (j t) -> p j t", j=4)[:, :, :nt]),
                    reads=[psb[bi]], writes=wr)
    S.barrier()
    free(4)

    def rms_stats_full(sqt, sqb, rt, rtb):
        sbanks = [5, 6, 7]
        for kc in range(KC):
            q, qb = sqt[kc % 2], sqb[kc % 2]
            S.op("act", lambda e, kc=kc, q=q: e.activation(out=q[:], in_=x[:, kc, :], func=AF.Square),
                 reads=xb[kc], writes=[qb])
            fns = []
            for bl, (c0, n, _) in enumerate(BLKS):
                fns.append(lambda e, bl=bl, c0=c0, n=n, q=q, kc=kc: e.matmul(
                    ps[:, sbanks[bl], :n], lhsT=ones_b[:], rhs=q[:, c0:c0 + n], start=(kc == 0), stop=(kc == KC - 1)))
            S.mm(fns, reads=[qb, b_ones], writes=[psb[b] for b in sbanks])
        for bl, (c0, n, _) in enumerate(BLKS):
            S.op("act", lambda e, bl=bl, c0=c0, n=n: e.activation(
                out=rt[:, c0:c0 + n], in_=ps[:, sbanks[bl], :n], func=AF.Sqrt, scale=1.0 / D, bias=EPS),
                reads=[psb[sbanks[bl]]], writes=[rtb])
        S.op("dve", lambda e: e.reciprocal(out=rt[:], in_=rt[:]), reads=[rtb], writes=[rtb])

    def rms_block(c0, n, segs, gcol, h, hb, sq, sqb, rt, rtb):
        xs = lambda kc: [xb[kc][s] for s in segs]
        pieces = [(0, n)] if n <= 512 else [(0, 512), (512, n - 512)]
        pbank = [7, 6]
        for kc in range(KC):
            q, qb = sq[kc % 2], sqb[kc % 2]
            S.op("act", lambda e, kc=kc, q=q: e.activation(out=q[:, :n], in_=x[:, kc, c0:c0 + n], func=AF.Square),
                 reads=xs(kc), writes=[qb])
            S.mm([lambda e, kc=kc, q=q, pi=pi, o=o, w=w: e.matmul(ps[:, pbank[pi], :w], lhsT=ones_b[:], rhs=q[:, o:o + w],
                                                              start=(kc == 0), stop=(kc == KC - 1))
                  for pi, (o, w) in enumerate(pieces)],
                 reads=[qb, b_ones], writes=[psb[pbank[pi]] for pi in range(len(pieces))])
        for pi, (o, w) in enumerate(pieces):
            S.op("act", lambda e, pi=pi, o=o, w=w: e.activation(out=rt[:, o:o + w], in_=ps[:, pbank[pi], :w], func=AF.Sqrt,
                                                                scale=1.0 / D, bias=EPS),
                 reads=[psb[pbank[pi]]], writes=[rtb])
        S.op("dve", lambda e: e.reciprocal(out=rt[:, :n], in_=rt[:, :n]), reads=[rtb], writes=[rtb])
        for kc in range(KC):
            S.op("dve", lambda e, kc=kc: e.scalar_tensor_tensor(
                out=h[:, kc, :n], in0=x[:, kc, c0:c0 + n], scalar=vcol(gcol + kc), in1=rt[:, :n],
                op0=ALU.mult, op1=ALU.mult),
                reads=xs(kc) + [rtb, b_vec], writes=[hb[kc]])

    def ffn(i, j, blks=BLKS):
        h = alloc("h", [128, KC, T], BF16)
        hb = bufs(KC)
        sqt = [alloc("sq%d" % a, [128, T], BF16) for a in range(2)]
        sqb = bufs(2)
        rt = alloc("rt", [128, T], F32)
        rtb = Buf()
        hid = [alloc("hid%d" % a, [128, 4, T], BF16) for a in range(2)]
        hidb = [bufs(4), bufs(4)]
        sg = [alloc("sg%d" % a, [128, 512], BF16) for a in range(3)]
        sgb = bufs(3)
        sgi = [0]
        gcol = (i * 2 + j) * 16
        rms_stats_full(sqt, sqb, rt, rtb)
        for kc in range(KC):
            S.op("dve", lambda e, kc=kc: e.scalar_tensor_tensor(
                out=h[:, kc, :], in0=x[:, kc, :], scalar=vcol(gcol + kc), in1=rt[:],
                op0=ALU.mult, op1=ALU.mult),
                reads=xb[kc] + [rtb, b_vec], writes=[hb[kc]])
        for g in range(NG + 1):
            for hf in range(2):
                if g < NG:
                    sg_slot = acquire(("gate", i, j, g, hf))
                    su_slot = acquire(("up", i, j, g, hf))
                if g > 0:
                    sd_slot = acquire(("down", i, j, g - 1, hf))
                for mm_ in range(2):
                    m = hf * 2 + mm_
                    if g < NG:
                        hcur, hcb = hid[g % 2], hidb[g % 2]
                        for bl, (c0, n, _) in enumerate(blks):
                            bg = bank()
                            fns = [lambda e, kc=kc, bg=bg, c0=c0, n=n, sl=sg_slot, mm_=mm_: e.matmul(
                                ps[:, bg, :n], lhsT=ring_t[:, sl, kc * 256 + mm_ * 128:kc * 256 + mm_ * 128 + 128],
                                rhs=h[:, kc, c0:c0 + n], start=(kc == 0), stop=(kc == KC - 1)) for kc in range(KC)]
                            S.mm(fns, reads=hb, writes=[psb[bg]])
                            si = sgi[0] % 3
                            sgi[0] += 1
                            S.op("act", lambda e, bg=bg, n=n, si=si: e.activation(
                                out=sg[si][:, :n], in_=ps[:, bg, :n], func=AF.Silu),
                                reads=[psb[bg]], writes=[sgb[si]])
                            bu = bank()
                            fns = [lambda e, kc=kc, bu=bu, c0=c0, n=n, sl=su_slot, mm_=mm_: e.matmul(
                                ps[:, bu, :n], lhsT=ring_t[:, sl, kc * 256 + mm_ * 128:kc * 256 + mm_ * 128 + 128],
                                rhs=h[:, kc, c0:c0 + n], start=(kc == 0), stop=(kc == KC - 1)) for kc in range(KC)]
                            S.mm(fns, reads=hb, writes=[psb[bu]])
                            S.op("dve", lambda e, bu=bu, n=n, si=si, c0=c0, m=m, hcur=hcur: e.tensor_tensor(
                                out=hcur[:, m, c0:c0 + n], in0=ps[:, bu, :n], in1=sg[si][:, :n], op=ALU.mult),
                                reads=[psb[bu], sgb[si]], writes=[hcb[m]])
                    if g > 0:
                        hprev, hpb = hid[(g - 1) % 2], hidb[(g - 1) % 2]
                        for o4 in range(4):
                            oc = hf * 8 + mm_ * 4 + o4
                            ol = mm_ * 4 + o4
                            for bl, (c0, n, segs) in enumerate(blks):
                                bd = bank()
                                fns = [lambda e, hc=hc, bd=bd, c0=c0, n=n, sl=sd_slot, ol=ol, hprev=hprev: e.matmul(
                                    ps[:, bd, :n], lhsT=ring_t[:, sl, hc * 1024 + ol * 128:hc * 1024 + ol * 128 + 128],
                                    rhs=hprev[:, hc, c0:c0 + n], start=(hc == 0), stop=(hc == 3)) for hc in range(4)]
                                S.mm(fns, reads=hpb, writes=[psb[bd]])
                                xw = [xb[oc][s] for s in segs]
                                S.op("dve", lambda e, bd=bd, n=n, c0=c0, oc=oc: e.scalar_tensor_tensor(
                                    out=x[:, oc, c0:c0 + n], in0=ps[:, bd, :n], scalar=0.5, in1=x[:, oc, c0:c0 + n],
                                    op0=ALU.mult, op1=ALU.add),
                                    reads=[psb[bd]] + xw, writes=xw)
                release()
        S.barrier()
        free(9)

    def conv_mixer():
        conv_phase(False)
        conv_phase(True)

    def conv_phase(samp_phase):
        n0 = len(ctx)
        nb = TS if samp_phase else NBC
        wdw_t = alloc("wdw_t", [128, 16, 31], F32)
        b_wdw = Buf()
        S.dma("sp", wdw_t[:].rearrange("p c j -> p (c j)"), wdw_d[:, :], writes=[b_wdw])
        hbk = alloc("c_h", [128, KC, nb], BF16)
        hbb = bufs(KC)
        sq = [alloc("c_sq%d" % a, [128, nb], BF16) for a in range(2)]
        sqb = bufs(2)
        rt = alloc("c_rt", [128, nb], F32)
        rtb = Buf()
        sig = [alloc("c_sig%d" % a, [128, nb], F32) for a in range(2)]
        sigb = bufs(2)
        dg = [alloc("c_dg%d" % a, [128, 31, 128], BF16) for a in range(2)]
        dgb = bufs(2)
        y = alloc("c_y", [128, KC, nb], F32)
        yb = bufs(KC)
        ybf = [alloc("c_ybf%d" % a, [128, nb], BF16) for a in range(2)]
        ybfb = bufs(2)
        ysq = [alloc("c_ysq%d" % a, [128, nb], BF16) for a in range(2)]
        ysqb = bufs(2)
        mean = alloc("c_mean", [128, nb], F32)
        rstd = alloc("c_rstd", [128, nb], F32)
        mr = alloc("c_mr", [128, nb], F32)
        b_mean, b_rstd, b_mr = Buf(), Buf(), Buf()
        t1 = [alloc("c_t1%d" % a, [128, nb], F32) for a in range(2)]
        t1b = bufs(2)
        uf = alloc("c_uf", [128, KC, 64], F32)
        ufb = bufs(KC)
        if samp_phase:
            tko = alloc("c_tko", [64, D], F32)
            tko_alias = []
        else:
            tko = y[:64, 0:6, :].rearrange("p a b -> p (a b)")[:, 0:D]
            tko_alias = yb[0:6]
        tkob = Buf()

        def run_block(c0, n, segs, samp, npre, ext_w, ext_r, extb, last_main):
            rms_block(c0, n, segs, 64, hbk, hbb, sq, sqb, rt, rtb)

            def pw1(c):
                slot = acquire(("pw1", c))
                bv, bg_ = bank(POOL5), bank(POOL5)
                for half, bb in ((0, bv), (1, bg_)):
                    fns = [lambda e, kc=kc, bb=bb, half=half, slot=slot: e.matmul(
                        ps[:, bb, :n], lhsT=ring_t[:, slot, kc * 256 + half * 128:kc * 256 + half * 128 + 128],
                        rhs=hbk[:, kc, :n], start=(kc == 0), stop=(kc == KC - 1)) for kc in range(KC)]
                    S.mm(fns, reads=hbb, writes=[psb[bb]])
                release()
                si = c % 2
                S.op("act", lambda e: e.activation(out=sig[si][:, :n], in_=ps[:, bg_, :n], func=AF.Sigmoid,
                                                   bias=vcol(V_B1G + c)),
                     reads=[psb[bg_], b_vec], writes=[sigb[si]])
                if npre:
                    S.op("dve", lambda e: e.tensor_tensor(out=sig[si][:, :npre], in0=sig[si][:, :npre],
                                                          in1=cst_t[:, C_CM:C_CM + npre], op=ALU.mult),
                         reads=[sigb[si], b_cst], writes=[sigb[si]])
                S.op("dve", lambda e: e.scalar_tensor_tensor(
                    out=ext_w(c), in0=(ps[:, bv, :n].rearrange("p (s t) -> p s t", t=4) if samp else ps[:, bv, :n]),
                    scalar=vcol(V_B1V + c),
                    in1=(sig[si][:, :n].rearrange("p (s t) -> p s t", t=4) if samp else sig[si][:, :n]),
                    op0=ALU.add, op1=ALU.mult),
                    reads=[psb[bv], sigb[si], b_vec], writes=[extb[c]])
                if last_main or samp:
                    k0 = 0 if samp else n - 30
                    w = n - k0
                    S.op("dve", lambda e: e.scalar_tensor_tensor(
                        out=uf[:, c, :w], in0=ps[:, bv, k0:n], scalar=vcol(V_B1V + c), in1=sig[si][:, k0:n],
                        op0=ALU.add, op1=ALU.mult),
                        reads=[psb[bv], sigb[si], b_vec], writes=[ufb[c]])
                di = c % 2
                S.op("dve", lambda e: e.tensor_tensor(
                    out=dg[di][:], in0=ident.unsqueeze(1).to_broadcast([128, 31, 128]),
                    in1=wdw_t[:, c, :].unsqueeze(2).to_broadcast([128, 31, 128]), op=ALU.mult),
                    reads=[b_cst, b_wdw], writes=[dgb[di]])

            def dconv(c):
                di = c % 2
                by = bank(POOL5)
                fns = [lambda e, j=j: e.matmul(ps[:, by, :n], lhsT=dg[di][:, j, :], rhs=ext_r(c, j),
                                               start=(j == 0), stop=(j == 30)) for j in range(31)]
                S.mm(fns, reads=[dgb[di], extb[c]], writes=[psb[by]])
                S.op("act", lambda e: e.activation(out=y[:, c, :n], in_=ps[:, by, :n], func=AF.Identity,
                                                   bias=vcol(V_BDW + c)),
                     reads=[psb[by], b_vec], writes=[yb[c]])
                yi = c % 2
                S.op("act", lambda e: e.activation(out=ybf[yi][:, :n], in_=ps[:, by, :n], func=AF.Identity,
                                                   bias=vcol(V_BDW + c)),
                     reads=[psb[by], b_vec], writes=[ybfb[yi]])
                S.op("act", lambda e: e.activation(out=ysq[yi][:, :n], in_=ps[:, by, :n], func=AF.Square,
                                                   bias=vcol(V_BDW + c)),
                     reads=[psb[by], b_vec], writes=[ysqb[yi]])

            def stats(c):
                di = c % 2
                S.mm([lambda e: e.matmul(ps[:, 6, :n], lhsT=ones_b[:], rhs=ybf[di][:, :n],
                                         start=(c == 0), stop=(c == KC - 1)),
                      lambda e: e.matmul(ps[:, 5, :n], lhsT=ones_b[:], rhs=ysq[di][:, :n],
                                         start=(c == 0), stop=(c == KC - 1))],
                     reads=[ybfb[di], ysqb[di], b_ones], writes=[psb[6], psb[5]])

            for c in range(KC + 2):
                if c < KC:
                    pw1(c)
                if 1 <= c <= KC:
                    dconv(c - 1)
                if c >= 2:
                    stats(c - 2)
            S.op("act", lambda e: e.activation(out=mean[:, :n], in_=ps[:, 6, :n], func=AF.Identity, scale=1.0 / D),
                 reads=[psb[6]], writes=[b_mean])
            S.op("dve", lambda e: e.tensor_tensor(out=mr[:, :n], in0=mean[:, :n], in1=mean[:, :n], op=ALU.mult),
                 reads=[b_mean], writes=[b_mr])
            S.op("dve", lambda e: e.scalar_tensor_tensor(out=rstd[:, :n], in0=ps[:, 5, :n], scalar=1.0 / D,
                                                         in1=mr[:, :n], op0=ALU.mult, op1=ALU.subtract),
                 reads=[psb[5], b_mr], writes=[b_rstd])
            S.op("act", lambda e: e.activation(out=rstd[:, :n], in_=rstd[:, :n], func=AF.Sqrt, bias=EPS),
                 reads=[b_rstd], writes=[b_rstd])
            S.op("dve", lambda e: e.reciprocal(out=rstd[:, :n], in_=rstd[:, :n]), reads=[b_rstd], writes=[b_rstd])
            S.op("dve", lambda e: e.tensor_tensor(out=mr[:, :n], in0=mean[:, :n], in1=rstd[:, :n], op=ALU.mult),
                 reads=[b_mean, b_rstd], writes=[b_mr])
            for c in range(KC):
                ti = c % 2
                S.op("dve", lambda e: e.tensor_tensor(out=t1[ti][:, :n], in0=y[:, c, :n], in1=rstd[:, :n], op=ALU.mult),
                     reads=[yb[c], b_rstd], writes=[t1b[ti]])
                S.op("dve", lambda e: e.tensor_tensor(out=t1[ti][:, :n], in0=t1[ti][:, :n], in1=mr[:, :n],
                                                      op=ALU.subtract),
                     reads=[t1b[ti], b_mr], writes=[t1b[ti]])
                S.op("act", lambda e: e.activation(out=hbk[:, c, :n], in_=t1[ti][:, :n], func=AF.Silu,
                                                   scale=vcol(V_LNG + c), bias=vcol(V_LNB + c)),
                     reads=[t1b[ti], b_vec], writes=[hbb[c]])
            for o2 in range(8):
                slot = acquire(("cols", "conv_w_pw2", o2 * 256))
                for oo in range(2):
                    oc = o2 * 2 + oo
                    bo = bank(POOL5)
                    fns = [lambda e, kc=kc: e.matmul(
                        ps[:, bo, :n], lhsT=ring_t[:, slot, kc * 256 + oo * 128:kc * 256 + oo * 128 + 128],
                        rhs=hbk[:, kc, :n], start=(kc == 0), stop=(kc == KC - 1)) for kc in range(KC)]
                    S.mm(fns, reads=hbb, writes=[psb[bo]])
                    xw = [xb[oc][s] for s in segs]
                    S.op("dve", lambda e: e.scalar_tensor_tensor(
                        out=x[:, oc, c0:c0 + n], in0=ps[:, bo, :n], scalar=vcol(V_B2 + oc), in1=x[:, oc, c0:c0 + n],
                        op0=ALU.add, op1=ALU.add),
                        reads=[psb[bo], b_vec] + xw, writes=xw)
                release()

        def out_state(w, dst_fn):
            for k4 in range(4):
                bi = bank(POOL5)
                fns = [lambda e, j=j: e.transpose(ps[:w, bi, j * 128:(j + 1) * 128], uf[:, k4 * 4 + j, :w], ident[:, :])
                       for j in range(4)]
                S.mm(fns, reads=[ufb[k4 * 4 + j] for j in range(4)] + [b_cst], writes=[psb[bi]])
                S.op("act", lambda e: e.activation(out=tko[:w, k4 * 512:(k4 + 1) * 512], in_=ps[:w, bi, :], func=AF.Copy),
                     reads=[psb[bi]], writes=[tkob] + tko_alias)
            dst_fn()

        if not samp_phase:
            ext = alloc("c_ext", [128, KC, 30 + NBC], BF16)
            extb = bufs(KC)
            S.op("dve", lambda e: e.memset(ext[:, :, 0:30], 0.0), writes=extb)
            for bi_, (c0, n, npre) in enumerate(CONV_BLOCKS):
                last_main = bi_ == len(CONV_BLOCKS) - 1
                run_block(c0, n, segs_of(c0, n), False, npre, lambda c: ext[:, c, 30:30 + n], lambda c, j: ext[:, c, j:j + n],
                          extb, last_main)
                if not last_main:
                    S.op("act", lambda e: e.activation(out=ext[:, :, 0:30], in_=ext[:, :, n:n + 30], func=AF.Copy),
                         reads=extb, writes=extb)
            out_state(30, lambda: S.dma("sp", ncp_out[:, :], tko[:30, :], reads=[tkob], is_output=True))
        else:
            exs = alloc("c_exs", [128, KC, 16, 34], BF16)
            exsb = bufs(KC)
            stk = [alloc("c_stk%d" % a, [120, D], F32) for a in range(2)]
            stkb = bufs(2)
            for gq in range(4):
                tk, tb = stk[gq % 2], stkb[gq % 2]
                S.dma("sp", tk[:, :], sconv[gq * 4:(gq + 1) * 4, :, :].rearrange("s j d -> (s j) d"), writes=[tb])
                for k4 in range(4):
                    bi = bank(POOL5)
                    fns = [lambda e, j=j: e.transpose(ps[:, bi, j * 120:(j + 1) * 120],
                                                      tk[:, (k4 * 4 + j) * 128:(k4 * 4 + j + 1) * 128], ident[:120, :120])
                           for j in range(4)]
                    S.mm(fns, reads=[tb, b_cst], writes=[psb[bi]])
                    for j in range(4):
                        c = k4 * 4 + j
                        S.op("act", lambda e: e.activation(
                            out=exs[:, c, gq * 4:(gq + 1) * 4, 0:30],
                            in_=ps[:, bi, j * 120:(j + 1) * 120].rearrange("p (s t) -> p s t", s=4), func=AF.Copy),
                            reads=[psb[bi]], writes=[exsb[c]])
            c0, n = CS, TS
            run_block(c0, n, segs_of(c0, n), True, 0, lambda c: exs[:, c, :, 30:34], lambda c, j: exs[:, c, :, j:j + 4], exsb, False)

            def samp_dst():
                for s_ in range(16):
                    S.dma("sp", ncs_out[s_, 26:30, :], tko[s_ * 4:(s_ + 1) * 4, :], reads=[tkob], is_output=True)
            out_state(64, samp_dst)
        S.barrier()
        free(len(ctx) - n0)

    def gla_mixer():
        Sst = alloc("g_S", [128, 8, 512], F32)
        Sstb = bufs(8)
        gla_phase("A", Sst, Sstb)
        gla_phase("S", Sst, Sstb)
        gla_phase("B", Sst, Sstb)
        free(1)

    def gla_phase(mode, Sst, Sstb):
        samp_phase = mode == "S"
        n0 = len(ctx)
        nb = TS if samp_phase else (544 if mode == "A" else NBG)
        nch = 1 if samp_phase else (5 if mode == "A" else 3)
        nbB = 2 if mode == "A" else nb
        gh = alloc("g_h", [128, KC, nb], BF16)
        ghb = bufs(KC)
        sq = [alloc("g_sq%d" % a, [128, nb], BF16) for a in range(2)]
        sqb = bufs(2)
        rt = alloc("g_rt", [128, nb], F32)
        rtb = Buf()
        g1a = alloc("g_g1a", [32, nb], BF16)
        b_g1a = Buf()
        wg1 = alloc("g_wg1", [128, 16, 16], BF16)
        wg2 = alloc("g_wg2", [32, 1024], BF16)
        b_wg1, b_wg2 = Buf(), Buf()
        la = alloc("g_la", [128, nch, 1024], F32)
        lab = bufs(5)
        E = alloc("g_E", [128, nch, 256], F32)
        Eb = bufs(5)
        kh = alloc("g_kh", [128, max(2, nch), 256], BF16)
        khb = bufs(5)
        vt = alloc("g_vt", [128, nch, 512], BF16)
        vtb = bufs(5)
        ep = alloc("g_ep", [128, 2, nb], F32)
        en = alloc("g_en", [128, 2, nbB], F32)
        epb, enb = bufs(2), bufs(2)
        qt = alloc("g_qt", [128, 2, nbB], BF16)
        kt = alloc("g_kt", [128, 2, nbB], BF16)
        qtb, ktb = bufs(2), bufs(2)
        sc = alloc("g_sc", [128, (1 if mode == "A" else nch), 128], BF16)
        scb = bufs(5)
        oT = alloc("g_oT", [128, 4, nbB], F32)
        oTb = Buf()
        osq = [alloc("g_osq%d" % a, [128, nbB], BF16) for a in range(4)]
        osqb = bufs(4)
        grt = alloc("g_grt", [128, nbB], F32)
        b_grt = Buf()
        rr = alloc("g_r", [128, 4, nbB], BF16)
        rrb = bufs(4)
        og = alloc("g_og", [128, KC, nbB], BF16)
        ogb = bufs(KC)
        t1 = [alloc("g_t1%d" % a, [128, nbB], F32) for a in range(2)]
        t1b = bufs(2)
        if samp_phase:
            s0f = [alloc("g_s0f%d" % a, [128, 2, 512], F32) for a in range(4)]
            s0fb = bufs(4)
            snw = [alloc("g_snw%d" % a, [128, 2, 512], F32) for a in range(4)]
            snwb = bufs(4)
            Sbs = [alloc("g_Sbs%d" % a, [128, 2, 512], BF16) for a in range(2)]
            Sbsb = bufs(2)
            Sb = None
            Sbb = []
        else:
            Sb = alloc("g_Sb", [128, 2, 512], BF16)
            Sbb = bufs(2)
            if mode == "A":
                S.op("dve", lambda e: e.memset(Sst[:], 0.0), writes=Sstb)

        S.dma("pool", wg1[:].rearrange("p a b -> p (a b)"), wg1_d[:, :], writes=[b_wg1])
        S.dma("pool", wg2[:], wg2_d[:, :], writes=[b_wg2])
        S.op("dve", lambda e: e.memset(g1a[:], 1.0), writes=[b_g1a])

        def gla_block(c0, n, segs, samp, chunks3, phaseB, part="all"):
            chunks = [(o, nt) for (o, nt, _) in chunks3]
            cpre = [m for (_, _, m) in chunks3]
            GP = POOL5 if samp else POOL6
            Um = cst_t[:, C_US:C_US + 64] if samp else cst_t[:, C_U:C_U + 128]
            Mm = cst_t[:, C_MS:C_MS + 64] if samp else cst_t[:, C_M:C_M + 128]
            if part in ("all", "pro"):
                rms_block(c0, n, segs, 80, gh, ghb, sq, sqb, rt, rtb)
                for (po, pw) in ([(0, n)] if n <= 512 else [(0, 512), (512, n - 512)]):
                    bq = bank(GP)
                    S.mm([lambda e, kc=kc: e.matmul(ps[:16, bq, :pw], lhsT=wg1[:, kc, :], rhs=gh[:, kc, po:po + pw],
                                                    start=(kc == 0), stop=(kc == KC - 1)) for kc in range(KC)],
                         reads=ghb + [b_wg1], writes=[psb[bq]])
                    S.op("act", lambda e: e.activation(out=g1a[0:16, po:po + pw], in_=ps[:16, bq, :pw], func=AF.Copy),
                         reads=[psb[bq]], writes=[b_g1a])
                for ci, (o, nt) in enumerate(chunks):
                    for half in range(2):
                        bz = bank(GP)
                        S.mm([lambda e: e.matmul(ps[:nt, bz, :512], lhsT=g1a[0:17, o:o + nt],
                                                 rhs=wg2[0:17, half * 512:(half + 1) * 512], start=True, stop=True)],
                             reads=[b_g1a, b_wg2], writes=[psb[bz]])
                        dst = la[:nt, ci, half * 512:(half + 1) * 512]
                        S.op("act", lambda e: e.activation(out=dst, in_=ps[:nt, bz, :512], func=AF.Exp, scale=-1.0),
                             reads=[psb[bz]], writes=[lab[ci]])
                        S.op("act", lambda e: e.activation(out=dst, in_=dst, func=AF.Ln, bias=1.0),
                             reads=[lab[ci]], writes=[lab[ci]])
                        if cpre[ci]:
                            S.op("dve", lambda e: e.tensor_scalar(out=dst, in0=dst, scalar1=-1.0 / 16, scalar2=vec_t[:nt, V_GMT:V_GMT + 1],
                                                                  op0=ALU.mult, op1=ALU.mult),
                                 reads=[lab[ci], b_vec], writes=[lab[ci]])
                        else:
                            S.op("dve", lambda e: e.tensor_scalar(out=dst, in0=dst, scalar1=-1.0 / 16, scalar2=None,
                                                                  op0=ALU.mult),
                                 reads=[lab[ci]], writes=[lab[ci]])
            if part == "pro":
                return
            for hd in (range(4) if part in ("all", "heads") else ()):
                if phaseB and not samp:
                    for dc in range(2):
                        S.op("act", lambda e: e.activation(out=Sb[:, dc, :], in_=Sst[:, hd * 2 + dc, :], func=AF.Copy),
                             reads=[Sstb[hd * 2 + dc]], writes=[Sbb[dc]])
                for ci, (o, nt) in enumerate(chunks):
                    be = bank(GP)
                    S.mm([lambda e: e.matmul(ps[:nt, be, :256], lhsT=Mm[:nt, :nt], rhs=la[:nt, ci, hd * 256:(hd + 1) * 256],
                                             start=True, stop=True)],
                         reads=[lab[ci], b_cst], writes=[psb[be]])
                    S.op("act", lambda e: e.activation(out=E[:nt, ci, :], in_=ps[:nt, be, :256], func=AF.Exp),
                         reads=[psb[be]], writes=[Eb[ci]])
                    if cpre[ci]:
                        S.op("dve", lambda e: e.tensor_scalar(out=E[:nt, ci, :], in0=E[:nt, ci, :],
                                                              scalar1=vec_t[:nt, V_GMT:V_GMT + 1], scalar2=None, op0=ALU.mult),
                             reads=[Eb[ci], b_vec], writes=[Eb[ci]])
                    for dc in range(2):
                        bd_ = bank(GP)
                        S.mm([lambda e: e.matmul(ps[:, bd_, :nt], lhsT=la[:nt, ci, hd * 256 + dc * 128:hd * 256 + dc * 128 + 128],
                                                 rhs=Um[:nt, :nt], start=True, stop=True)],
                             reads=[lab[ci], b_cst], writes=[psb[bd_]])
                        S.op("act", lambda e: e.activation(out=ep[:, dc, o:o + nt], in_=ps[:, bd_, :nt], func=AF.Exp),
                             reads=[psb[bd_]], writes=[epb[dc]])
                        if phaseB:
                            S.op("act", lambda e: e.activation(out=en[:, dc, o:o + nt], in_=ps[:, bd_, :nt], func=AF.Exp,
                                                               scale=-1.0),
                                 reads=[psb[bd_]], writes=[enb[dc]])
                sk = acquire(("cols", "gla_w_k", hd * 256))
                for ci, (o, nt) in enumerate(chunks):
                    bk = bank(GP)
                    S.mm([lambda e, kc=kc: e.matmul(ps[:nt, bk, :256], lhsT=gh[:, kc, o:o + nt],
                                                    rhs=ring_t[:, sk, kc * 256:(kc + 1) * 256],
                                                    start=(kc == 0), stop=(kc == KC - 1)) for kc in range(KC)],
                         reads=ghb, writes=[psb[bk]])
                    S.op("dve", lambda e: e.tensor_tensor(out=kh[:nt, ci, :], in0=ps[:nt, bk, :256], in1=E[:nt, ci, :],
                                                          op=ALU.mult),
                         reads=[psb[bk], Eb[ci]], writes=[khb[ci]])
                if phaseB:
                    for dc in range(2):
                        bk = bank(GP)
                        S.mm([lambda e, kc=kc: e.matmul(ps[:, bk, :n], lhsT=ring_t[:, sk, kc * 256 + dc * 128:kc * 256 + dc * 128 + 128],
                                                        rhs=gh[:, kc, :n], start=(kc == 0), stop=(kc == KC - 1))
                              for kc in range(KC)],
                             reads=ghb, writes=[psb[bk]])
                        S.op("dve", lambda e: e.tensor_tensor(out=kt[:, dc, :n], in0=ps[:, bk, :n], in1=en[:, dc, :n],
                                                              op=ALU.mult),
                             reads=[psb[bk], enb[dc]], writes=[ktb[dc]])
                        for ci, (o, nt) in enumerate(chunks):
                            if cpre[ci]:
                                S.op("dve", lambda e: e.tensor_tensor(out=kt[:, dc, o:o + nt], in0=kt[:, dc, o:o + nt],
                                                                      in1=cst_t[:, C_GM:C_GM + nt], op=ALU.mult),
                                     reads=[ktb[dc], b_cst], writes=[ktb[dc]])
                release()
                for half in range(2):
                    sv = acquire(("cols", "gla_w_v", hd * 512 + half * 256))
                    for ci, (o, nt) in enumerate(chunks):
                        bv = bank(GP)
                        S.mm([lambda e, kc=kc: e.matmul(ps[:nt, bv, :256], lhsT=gh[:, kc, o:o + nt],
                                                        rhs=ring_t[:, sv, kc * 256:(kc + 1) * 256],
                                                        start=(kc == 0), stop=(kc == KC - 1)) for kc in range(KC)],
                             reads=ghb, writes=[psb[bv]])
                        S.op("act", lambda e: e.activation(out=vt[:nt, ci, half * 256:(half + 1) * 256],
                                                           in_=ps[:nt, bv, :256], func=AF.Copy),
                             reads=[psb[bv]], writes=[vtb[ci]])
                    release()
                if phaseB:
                    sq_ = acquire(("cols", "gla_w_q", hd * 256))
                    for dc in range(2):
                        bq_ = bank(GP)
                        S.mm([lambda e, kc=kc: e.matmul(ps[:, bq_, :n], lhsT=ring_t[:, sq_, kc * 256 + dc * 128:kc * 256 + dc * 128 + 128],
                                                        rhs=gh[:, kc, :n], start=(kc == 0), stop=(kc == KC - 1))
                              for kc in range(KC)],
                             reads=ghb, writes=[psb[bq_]])
                        S.op("dve", lambda e: e.scalar_tensor_tensor(out=qt[:, dc, :n], in0=ps[:, bq_, :n], scalar=0.0625,
                                                                     in1=ep[:, dc, :n], op0=ALU.mult, op1=ALU.mult),
                             reads=[psb[bq_], epb[dc]], writes=[qtb[dc]])
                    release()
                    for half in range(2):
                        sr = acquire(("cols", "gla_w_r", hd * 512 + half * 256))
                        for vv in range(2):
                            vc = half * 2 + vv
                            br = bank(GP)
                            S.mm([lambda e, kc=kc: e.matmul(ps[:, br, :n], lhsT=ring_t[:, sr, kc * 256 + vv * 128:kc * 256 + vv * 128 + 128],
                                                            rhs=gh[:, kc, :n], start=(kc == 0), stop=(kc == KC - 1))
                                  for kc in range(KC)],
                                 reads=ghb, writes=[psb[br]])
                            S.op("act", lambda e: e.activation(out=rr[:, vc, :n], in_=ps[:, br, :n], func=AF.Silu),
                                 reads=[psb[br]], writes=[rrb[vc]])
                        release()
                    for ci, (o, nt) in enumerate(chunks):
                        bs_ = bank(GP)
                        S.mm([lambda e, dc=dc: e.matmul(ps[:nt, bs_, :nt], lhsT=kt[:, dc, o:o + nt], rhs=qt[:, dc, o:o + nt],
                                                        start=(dc == 0), stop=(dc == 1)) for dc in range(2)],
                             reads=ktb + qtb, writes=[psb[bs_]])
                        S.op("dve", lambda e: e.tensor_tensor(out=sc[:nt, ci, :nt], in0=ps[:nt, bs_, :nt], in1=Um[:nt, :nt],
                                                              op=ALU.mult),
                             reads=[psb[bs_], b_cst], writes=[scb[ci]])
                if samp:
                    BO = 5
                    S.mm([lambda e, vc=vc: e.matmul(ps[:, BO, vc * 128:vc * 128 + 64], lhsT=vt[:64, 0, vc * 128:(vc + 1) * 128],
                                                    rhs=sc[:64, 0, :64], start=(vc == 0), stop=False, skip_group_check=True)
                          for vc in range(4)],
                         reads=[vtb[0], scb[0]], writes=[psb[BO]])
                    def ld_state(q_):
                        S.dma("sp", s0f[q_ % 4][:], sgla[q_, hd, :, :].rearrange("(c p) v -> p c v", p=128), writes=[s0fb[q_ % 4]])
                    for q_ in range(3):
                        ld_state(q_)
                    for s_ in range(16):
                        bi2 = s_ % 4
                        sbi = s_ % 2
                        if s_ + 3 < 16:
                            ld_state(s_ + 3)
                        S.op("act", lambda e: e.activation(out=Sbs[sbi][:], in_=s0f[bi2][:], func=AF.Copy),
                             reads=[s0fb[bi2]], writes=[Sbsb[sbi]])
                        fns = []
                        for vc in range(4):
                            for dc in range(2):
                                last = (s_ == 15 and vc == 3 and dc == 1)
                                fns.append(lambda e, vc=vc, dc=dc, last=last: e.matmul(
                                    ps[:, BO, vc * 128 + s_ * 4:vc * 128 + s_ * 4 + 4], lhsT=Sbs[sbi][:, dc, vc * 128:(vc + 1) * 128],
                                    rhs=qt[:, dc, s_ * 4:s_ * 4 + 4], start=False, stop=last, skip_group_check=True))
                        S.mm(fns, reads=[Sbsb[sbi]] + qtb, writes=[psb[BO]])
                        S.op("dve", lambda e: e.tensor_scalar(out=kh[:64, 1, :], in0=kh[:64, 0, :],
                                                              scalar1=cst_t[:64, C_SEQ + s_:C_SEQ + s_ + 1], scalar2=None, op0=ALU.mult),
                             reads=[khb[0], b_cst], writes=[khb[1]])
                        for dc in range(2):
                            bu_ = bank(GP)
                            S.mm([lambda e: e.matmul(ps[:, bu_, :512], lhsT=kh[:64, 1, dc * 128:(dc + 1) * 128], rhs=vt[:64, 0, :],
                                                     start=True, stop=True)],
                                 reads=[khb[1], vtb[0]], writes=[psb[bu_]])
                            S.op("dve", lambda e: e.scalar_tensor_tensor(
                                out=snw[bi2][:, dc, :], in0=s0f[bi2][:, dc, :], scalar=ep[:, dc, s_ * 4 + 3:s_ * 4 + 4],
                                in1=ps[:, bu_, :512], op0=ALU.mult, op1=ALU.add),
                                reads=[s0fb[bi2], epb[dc], psb[bu_]], writes=[snwb[bi2]])
                        S.dma("sp", ngs_out[s_, hd, :, :].rearrange("(c p) v -> p c v", p=128), snw[bi2][:],
                              reads=[snwb[bi2]], is_output=True)
                    S.op("act", lambda e: e.activation(out=oT[:, :, :64],
                                                       in_=ps[:, BO, :].rearrange("p (v t) -> p v t", v=4)[:, :, :64], func=AF.Copy),
                         reads=[psb[BO]], writes=[oTb])
                else:
                    for ci, (o, nt) in enumerate(chunks):
                        if phaseB:
                            bo = bank(GP)
                            fns = []
                            for vc in range(4):
                                for dc in range(2):
                                    fns.append(lambda e, vc=vc, dc=dc: e.matmul(
                                        ps[:, bo, vc * 128:vc * 128 + nt], lhsT=Sb[:, dc, vc * 128:(vc + 1) * 128],
                                        rhs=qt[:, dc, o:o + nt], start=(vc == 0 and dc == 0), stop=False, skip_group_check=True))
                                fns.append(lambda e, vc=vc: e.matmul(
                                    ps[:, bo, vc * 128:vc * 128 + nt], lhsT=vt[:nt, ci, vc * 128:(vc + 1) * 128],
                                    rhs=sc[:nt, ci, :nt], start=False, stop=(vc == 3), skip_group_check=True))
                            S.mm(fns, reads=Sbb + qtb + [vtb[ci], scb[ci]], writes=[psb[bo]])
                            S.op("act", lambda e: e.activation(
                                out=oT[:, :, o:o + nt], in_=ps[:, bo, :].rearrange("p (v t) -> p v t", v=4)[:, :, :nt], func=AF.Copy),
                                reads=[psb[bo]], writes=[oTb])
                        for dc in range(2):
                            bu_ = bank(GP)
                            S.mm([lambda e: e.matmul(ps[:, bu_, :512], lhsT=kh[:nt, ci, dc * 128:(dc + 1) * 128], rhs=vt[:nt, ci, :],
                                                     start=True, stop=True)],
                                 reads=[khb[ci], vtb[ci]], writes=[psb[bu_]])
                            S.op("dve", lambda e: e.scalar_tensor_tensor(
                                out=Sst[:, hd * 2 + dc, :], in0=Sst[:, hd * 2 + dc, :], scalar=ep[:, dc, o + nt - 1:o + nt],
                                in1=ps[:, bu_, :512], op0=ALU.mult, op1=ALU.add),
                                reads=[Sstb[hd * 2 + dc], epb[dc], psb[bu_]], writes=[Sstb[hd * 2 + dc]])
                            if phaseB and ci + 1 < len(chunks):
                                S.op("act", lambda e: e.activation(out=Sb[:, dc, :], in_=Sst[:, hd * 2 + dc, :], func=AF.Copy),
                                     reads=[Sstb[hd * 2 + dc]], writes=[Sbb[dc]])
                if not phaseB:
                    continue
                for vc in range(4):
                    S.op("act", lambda e: e.activation(out=osq[vc][:, :n], in_=oT[:, vc, :n], func=AF.Square),
                         reads=[oTb], writes=[osqb[vc]])
                S.mm([lambda e, vc=vc: e.matmul(ps[:, 6, :n], lhsT=ones_b[:], rhs=osq[vc][:, :n], start=(vc == 0), stop=(vc == 3))
                      for vc in range(4)],
                     reads=osqb + [b_ones], writes=[psb[6]])
                S.op("act", lambda e: e.activation(out=grt[:, :n], in_=ps[:, 6, :n], func=AF.Sqrt, scale=1.0 / 512, bias=EPS),
                     reads=[psb[6]], writes=[b_grt])
                S.op("dve", lambda e: e.reciprocal(out=grt[:, :n], in_=grt[:, :n]), reads=[b_grt], writes=[b_grt])
                for vc in range(4):
                    ti = vc % 2
                    S.op("dve", lambda e: e.scalar_tensor_tensor(out=t1[ti][:, :n], in0=oT[:, vc, :n], scalar=vcol(V_GN + vc),
                                                                 in1=grt[:, :n], op0=ALU.mult, op1=ALU.mult),
                         reads=[oTb, b_grt, b_vec], writes=[t1b[ti]])
                    S.op("dve", lambda e: e.tensor_tensor(out=og[:, hd * 4 + vc, :n], in0=t1[ti][:, :n], in1=rr[:, vc, :n],
                                                          op=ALU.mult),
                         reads=[t1b[ti], rrb[vc]], writes=[ogb[hd * 4 + vc]])
            if not phaseB or part == "heads":
                return
            for o2 in range(8):
                so = acquire(("cols", "gla_w_o", o2 * 256))
                for oo in range(2):
                    oc = o2 * 2 + oo
                    bo = bank(GP)
                    S.mm([lambda e, kc=kc: e.matmul(ps[:, bo, :n], lhsT=ring_t[:, so, kc * 256 + oo * 128:kc * 256 + oo * 128 + 128],
                                                    rhs=og[:, kc, :n], start=(kc == 0), stop=(kc == KC - 1)) for kc in range(KC)],
                         reads=ogb, writes=[psb[bo]])
                    xw = [xb[oc][s] for s in segs]
                    S.op("dve", lambda e: e.tensor_tensor(out=x[:, oc, c0:c0 + n], in0=ps[:, bo, :n], in1=x[:, oc, c0:c0 + n],
                                                          op=ALU.add),
                         reads=[psb[bo]] + xw, writes=xw)
                release()

        if mode == "A":
            for (c0, n, ch3) in GLA_BLOCKS_A:
                gla_block(c0, n, segs_of(c0, n), False, ch3, False)
            ev = S.dma("pool", cc_src.ap().rearrange("(g p) v -> p g v", p=128), Sst[:], reads=Sstb)
            S.wait_event("pool", ev)
            nc.gpsimd.collective_compute(
                "AllGather", ALU.bypass, replica_groups=[[0, 1], [2, 3], [4, 5], [6, 7]],
                ins=[cc_src.ap().opt()], outs=[cc_dst.ap().opt()]).then_inc(S.semh["cc"])
            S.cnt["cc"] += 1
        elif mode == "S":
            gla_block(CS, TS, segs_of(CS, TS), True, ((0, TS, False),), True)
        else:
            S.wait_event("pool", ("cc", S.cnt["cc"]))
            S.dma("pool", Sst[:], cc_dst.ap()[0:1024, :].rearrange("(g p) v -> p g v", p=128), reads=Sstb, writes=Sstb)
            S.op("dve", lambda e: e.tensor_scalar(out=Sst[:], in0=Sst[:], scalar1=vcol(V_FLAG), scalar2=None, op0=ALU.mult),
                 reads=Sstb + [b_vec], writes=Sstb)
            for bi_, (c0, n, ch3) in enumerate(GLA_BLOCKS):
                if bi_ == 0:
                    gla_block(c0, n, segs_of(c0, n), False, ch3, True, "pro")
                gla_block(c0, n, segs_of(c0, n), False, ch3, True, "heads")
                if bi_ + 1 < len(GLA_BLOCKS):
                    c1, n1, ch1 = GLA_BLOCKS[bi_ + 1]
                    gla_block(c1, n1, segs_of(c1, n1), False, ch1, True, "pro")
                gla_block(c0, n, segs_of(c0, n), False, ch3, True, "wo")
            S.dma("sp", ngp_out.rearrange("(g p) v -> p g v", p=128), Sst[:], reads=Sstb, is_output=True)
        S.barrier()
        free(len(ctx) - n0)

    def final_out():
        gcol = 96
        sqt = [alloc("fsq%d" % a, [128, T], BF16) for a in range(2)]
        sqb = bufs(2)
        rt = alloc("frt", [128, T], F32)
        rtb = Buf()
        yt = alloc("yt", [128, KC, 128], F32)
        ytb = bufs(KC)
        tko = [alloc("tko%d" % a, [128, D], F32) for a in range(2)]
        tkb = bufs(2)
        rms_stats_full(sqt, sqb, rt, rtb)
        tpool = (0, 1, 2, 3, 4)
        for ti, (r0, nt) in enumerate(TT):
            for kc in range(KC):
                S.op("dve", lambda e, kc=kc: e.scalar_tensor_tensor(
                    out=yt[:, kc, :nt], in0=x[:, kc, r0:r0 + nt], scalar=vcol(gcol + kc),
                    in1=rt[:, r0:r0 + nt], op0=ALU.mult, op1=ALU.mult),
                    reads=xb[kc] + [rtb, b_vec], writes=[ytb[kc]])
            tk, tb = tko[ti % 2], tkb[ti % 2]
            for k4 in range(4):
                bi = bank(tpool)
                fns = [lambda e, j=j: e.transpose(ps[:nt, bi, j * 128:(j + 1) * 128], yt[:, k4 * 4 + j, :nt], ident[:, :])
                       for j in range(4)]
                S.mm(fns, reads=[ytb[k4 * 4 + j] for j in range(4)] + [b_cst], writes=[psb[bi]])
                S.op("act", lambda e: e.activation(out=tk[:nt, k4 * 512:(k4 + 1) * 512], in_=ps[:nt, bi, :], func=AF.Copy),
                     reads=[psb[bi]], writes=[tb])
            S.dma("sp", y_out[r0:r0 + nt, :], tk[:nt, :], reads=[tb], is_output=True)
        for ev in S.out_events:
            S.wait_event("sp", ev)
        free(6)

    stages = STAGES.split(",")
    full = "all" in stages
    if full or "ffn" in stages:
        ffn(0, 0)
    if full or "conv" in stages:
        conv_mixer()
        S.dma("act", ncs_out[:, 0:26, :], sconv[:, 4:30, :], is_output=True)
    if full:
        ffn(0, 1)
        ffn(1, 0)
    if full or "gla" in stages:
        gla_mixer()
    if full:
        ffn(1, 1, [(TP, 512 - TP, segs_of(TP, 512 - TP))] + BLKS[1:])
    final_out()
    pcm.__exit__(None, None, None)
    while ctx:
        free(1)
    S.close()
    if record:
        return rec_keys
    assert ring.n_acq == len(plan_keys), (ring.n_acq, len(plan_keys))
    return nc


_CACHE = {}


def _get_prog():
    if "prog" not in _CACHE:
        keys = build_program(None)
        nc = build_program(keys)
        uniq = {}
        for k in keys:
            if k[0] != "state" and k not in uniq:
                uniq[k] = len(uniq)
        _CACHE["prog"] = (nc, uniq)
    return _CACHE["prog"]


def _vec16(v):
    return np.asarray(v, np.float32).reshape(16, 128).T


def kernel(**inputs):
    inp = {k: np.asarray(v) for k, v in inputs.items()}
    nc, uniq = _get_prog()
    wpack = np.zeros((max(1, len(uniq)), 128, UE), np.float32)
    for key, idx in uniq.items():
        wpack[idx] = materialize(key, inp)
    vecs0 = np.zeros((128, 256), np.float32)
    for i in range(2):
        for j in range(2):
            vecs0[:, (i * 2 + j) * 16:(i * 2 + j) * 16 + 16] = _vec16(inp["norm_ffn"][i, j])
    vecs0[:, 64:80] = _vec16(inp["norm_mix"][0])
    vecs0[:, 80:96] = _vec16(inp["norm_mix"][1])
    vecs0[:, 96:112] = _vec16(inp["norm_final"])
    vecs0[:, V_B1V:V_B1V + 16] = _vec16(inp["conv_b_pw1"][0, :2048])
    vecs0[:, V_B1G:V_B1G + 16] = _vec16(inp["conv_b_pw1"][0, 2048:])
    vecs0[:, V_BDW:V_BDW + 16] = _vec16(inp["conv_b_dw"][0])
    vecs0[:, V_LNG:V_LNG + 16] = _vec16(inp["conv_ln_g"][0])
    vecs0[:, V_LNB:V_LNB + 16] = _vec16(inp["conv_ln_b"][0])
    vecs0[:, V_B2:V_B2 + 16] = _vec16(inp["conv_b_pw2"][0])
    vecs0[:, V_GN:V_GN + 4] = np.asarray(inp["gla_gn_g"][0], np.float32).reshape(4, 128).T
    cst0 = np.zeros((128, 1024), np.float32)
    ii = np.arange(128)
    cst0[:, C_ID:C_ID + 128] = np.eye(128, dtype=np.float32)
    cst0[:, C_U:C_U + 128] = (ii[:, None] <= ii[None, :])
    cst0[:, C_M:C_M + 128] = (ii[:, None] > ii[None, :])
    i64 = np.arange(64)
    same = (i64[:, None] // 4) == (i64[None, :] // 4)
    cst0[:64, C_US:C_US + 64] = same & (i64[:, None] <= i64[None, :])
    cst0[:64, C_MS:C_MS + 64] = same & (i64[:, None] > i64[None, :])
    cst0[:64, C_SEQ:C_SEQ + 16] = (i64[:, None] // 4) == np.arange(16)[None, :]
    wdw = np.ascontiguousarray(inp["conv_w_dw"][0].reshape(31, 16, 128).transpose(2, 1, 0)).reshape(128, 496)
    wg1 = np.ascontiguousarray(inp["gla_w_g1"][0].reshape(16, 128, 16).transpose(1, 0, 2)).reshape(128, 256)
    wg2a = np.zeros((32, 1024), np.float32)
    wg2a[0:16] = inp["gla_w_g2"][0]
    wg2a[16] = inp["gla_b_g"][0]
    xp, xs, meta = inp["x_prompt"], inp["x_sample"], inp["meta_tokens"]
    in_maps = []
    for c in range(8):
        b, hf = c // 2, c % 2
        xc = np.zeros((T, D), np.float32)
        xc[CM:CM + TM] = xp[b, hf * TM:(hf + 1) * TM]
        vecs = vecs0.copy()
        cst = cst0.copy()
        if hf == 0:
            xc[CP + TP - 16:CP + TP] = meta
            cst[:, C_CM + TP - 16:C_CM + TP] = 1.0
            cst[:, C_GM + TP - 16:C_GM + TP] = 1.0
            vecs[TP - 16:TP, V_GMT] = 1.0
        else:
            xc[CP:CP + TP] = xp[b, TM - TP:TM]
            cst[:, C_CM:C_CM + TP] = 1.0
            vecs[:, V_FLAG] = 1.0
        xc[CS:CS + TS] = xs[16 * c:16 * c + 16].reshape(TS, D)
        in_maps.append({
            "x_in": xc, "vecs": vecs, "cst": cst, "wdw": wdw, "wg1": wg1, "wg2a": wg2a, "wpack": wpack,
            "sconv": np.ascontiguousarray(inp["state_conv"][0, 16 * c:16 * c + 16]),
            "sgla": np.ascontiguousarray(inp["state_gla"][0, 16 * c:16 * c + 16]),
        })
    res = run_bass_kernel_spmd(nc, in_maps, core_ids=list(range(8)))
    yp = np.zeros((4, 2048, D), np.float32)
    ys = np.zeros((128, 4, D), np.float32)
    ncp = np.zeros((1, 4, 30, D), np.float32)
    ngp = np.zeros((1, 4, 4, 256, 512), np.float32)
    ncs = np.zeros((1, 128, 30, D), np.float32)
    ngs = np.zeros((1, 128, 4, 256, 512), np.float32)
    for c in range(8):
        b, hf = c // 2, c % 2
        r = res.results[c]
        yo = r["y_out"]
        yp[b, hf * TM:(hf + 1) * TM] = yo[CM:CM + TM]
        ys[16 * c:16 * c + 16] = yo[CS:CS + TS].reshape(16, 4, D)
        if hf == 1:
            ncp[0, b] = r["ncp_out"]
            ngp[0, b] = r["ngp_out"].reshape(4, 256, 512)
        ncs[0, 16 * c:16 * c + 16] = r["ncs_out"]
        ngs[0, 16 * c:16 * c + 16] = r["ngs_out"]
    return yp, ys, ncp, ngp, ncs, ngs
```
